# Optimizing a Trainium2 kernel written in Bass

```python
import jax, jax.numpy as jnp
from jax import lax
import numpy as np

D_MODEL = 2048
BATCH = 8
SEQ = 4096
DEPTH = 4
DEC_BATCH = 4
DEC_SEQ = 8192
PAST_LEN = 128

RET_HEADS = 8
RET_HEAD_DIM = 128
RET_WIDTH = RET_HEADS * RET_HEAD_DIM
LRU_BLOCKS = 8
LRU_BLOCK = 128
LRU_WIDTH = LRU_BLOCKS * LRU_BLOCK
MIX_WIDTH = RET_WIDTH + LRU_WIDTH
IN_WIDTH = 4 * RET_WIDTH + 2 * LRU_WIDTH
CONV_WIDTH = 4
CONV_PAD_LEFT = 2
CONV_PAD_RIGHT = CONV_WIDTH - 1 - CONV_PAD_LEFT
LRU_C = 8.0
D_FF = -(-8 * D_MODEL // (3 * 256)) * 256
CHUNK = 128
ROPE_BASE = 10000.0
EPS = 1e-6

kernel_name = "hymba_retnet_rglru_bidir_encoder"


def _rmsnorm(x, gain):
    xf = x.astype(jnp.float32)
    y = xf * lax.rsqrt(jnp.mean(xf * xf, axis=-1, keepdims=True) + EPS)
    return (y * gain.astype(jnp.float32)).astype(x.dtype)


def _rope(x):
    S, d = x.shape[1], x.shape[-1]
    inv = ROPE_BASE ** (-jnp.arange(0, d, 2, dtype=jnp.float32) / d)
    ang = jnp.arange(S, dtype=jnp.float32)[:, None] * inv[None, :]
    cos = jnp.cos(ang)[None, :, None, :].astype(x.dtype)
    sin = jnp.sin(ang)[None, :, None, :].astype(x.dtype)
    x1, x2 = x[..., : d // 2], x[..., d // 2:]
    return jnp.concatenate([x1 * cos - x2 * sin, x2 * cos + x1 * sin], axis=-1)


def _retention(q, k, v):
    B, S, H, dk = q.shape
    dv = v.shape[-1]
    N = S // CHUNK
    dt = q.dtype
    q = q.reshape(B, N, CHUNK, H, dk)
    k = k.reshape(B, N, CHUNK, H, dk)
    v = v.reshape(B, N, CHUNK, H, dv)
    log_g = jnp.log1p(-(2.0 ** (-5.0 - jnp.arange(H, dtype=jnp.float32))))
    pos = jnp.arange(CHUNK, dtype=jnp.float32)
    d_intra = jnp.exp(log_g[:, None, None] * jnp.abs(pos[:, None] - pos[None, :])[None]).astype(dt)
    scores = jnp.einsum('bnihd,bnjhd->bnhij', q, k) * d_intra
    y = jnp.einsum('bnhij,bnjhe->bnihe', scores, v)
    k_f = k * jnp.exp(log_g[None, :] * (CHUNK - 1 - pos)[:, None]).astype(dt)[:, :, None]
    k_b = k * jnp.exp(log_g[None, :] * pos[:, None]).astype(dt)[:, :, None]
    kv_f = jnp.einsum('bnjhd,bnjhe->nbhde', k_f, v)
    kv_b = jnp.einsum('bnjhd,bnjhe->nbhde', k_b, v)
    g_chunk = jnp.exp(log_g * CHUNK).astype(dt)[None, :, None, None]

    def step(state, kv):
        return state * g_chunk + kv, state

    zero = jnp.zeros((B, H, dk, dv), dt)
    _, s_f = lax.scan(step, zero, kv_f)
    _, s_b = lax.scan(step, zero, kv_b, reverse=True)
    q_f = q * jnp.exp(log_g[None, :] * (pos + 1.0)[:, None]).astype(dt)[:, :, None]
    q_b = q * jnp.exp(log_g[None, :] * (CHUNK - pos)[:, None]).astype(dt)[:, :, None]
    y = (y + jnp.einsum('bnihd,nbhde->bnihe', q_f, s_f)
         + jnp.einsum('bnihd,nbhde->bnihe', q_b, s_b))
    return y.reshape(B, S, H, dv)


def _centred_conv(u, w, bias):
    S = u.shape[1]
    up = jnp.pad(u, ((0, 0), (CONV_PAD_LEFT, CONV_PAD_RIGHT), (0, 0)))
    return bias + sum(up[:, t:t + S] * w[t] for t in range(CONV_WIDTH))


def _rg_lru_terms(u, w_a, b_a, w_x, b_x, lam):
    B, S, W = u.shape
    ub = u.reshape(B, S, LRU_BLOCKS, LRU_BLOCK)
    r = jax.nn.sigmoid(jnp.einsum('bsgi,gij->bsgj', ub, w_a).reshape(B, S, W) + b_a)
    i = jax.nn.sigmoid(jnp.einsum('bsgi,gij->bsgj', ub, w_x).reshape(B, S, W) + b_x)
    log_a = -LRU_C * r * jax.nn.softplus(-lam)
    a = jnp.exp(log_a)
    b = jnp.sqrt(-jnp.expm1(2.0 * log_a)) * (i * u)
    return a, b


def _combine(e1, e2):
    a1, b1 = e1
    a2, b2 = e2
    return a1 * a2, a2 * b1 + b2


def _linear_scan(a, b, reverse):
    if reverse:
        a, b = jnp.flip(a, 1), jnp.flip(b, 1)
    _, h = lax.associative_scan(_combine, (a, b), axis=1)
    return jnp.flip(h, 1) if reverse else h


def _layer(x, norm_pre_mix, w_in, ret_norm, conv_w, conv_b,
           lru_wa_fwd, lru_ba_fwd, lru_wx_fwd, lru_bx_fwd, lru_lam_fwd,
           lru_wa_bwd, lru_ba_bwd, lru_wx_bwd, lru_bx_bwd, lru_lam_bwd,
           lru_norm, w_out, norm_post_mix, norm_pre_ffn, w_gate, w_up, w_down, norm_post_ffn):
    B, S, _ = x.shape
    h = _rmsnorm(x, norm_pre_mix)
    proj = h @ w_in
    q, k, v, g, u_x, u_y = jnp.split(
        proj, [RET_WIDTH, 2 * RET_WIDTH, 3 * RET_WIDTH, 4 * RET_WIDTH, 4 * RET_WIDTH + LRU_WIDTH], axis=-1)
    q = _rope(q.reshape(B, S, RET_HEADS, RET_HEAD_DIM))
    k = _rope(k.reshape(B, S, RET_HEADS, RET_HEAD_DIM)) * (RET_HEAD_DIM ** -0.5)
    v = v.reshape(B, S, RET_HEADS, RET_HEAD_DIM)
    ret = _rmsnorm(_retention(q, k, v), ret_norm).reshape(B, S, RET_WIDTH)
    ret = jax.nn.silu(g) * ret
    uc = _centred_conv(u_x, conv_w, conv_b)
    a_f, b_f = _rg_lru_terms(uc, lru_wa_fwd, lru_ba_fwd, lru_wx_fwd, lru_bx_fwd, lru_lam_fwd)
    a_b, b_b = _rg_lru_terms(uc, lru_wa_bwd, lru_ba_bwd, lru_wx_bwd, lru_bx_bwd, lru_lam_bwd)
    lru = _linear_scan(a_f, b_f, False) + _linear_scan(a_b, b_b, True)
    lru = _rmsnorm(lru, lru_norm) * jax.nn.gelu(u_y)
    mix = jnp.concatenate([ret, lru], axis=-1) @ w_out
    x = x + _rmsnorm(mix, norm_post_mix)
    h = _rmsnorm(x, norm_pre_ffn)
    f = (jax.nn.silu(h @ w_gate) * (h @ w_up)) @ w_down
    return x + _rmsnorm(f, norm_post_ffn)


def _trunk(x, params):
    for l in range(DEPTH):
        x = _layer(x, *[p[l] for p in params])
    return x


def setup_inputs(seed: int = 0) -> dict:
    key = jax.random.key(seed)
    ks = jax.random.split(key, 32)
    f32 = jnp.float32
    nrm = lambda k, shape, s: jax.random.normal(k, shape, f32) * s
    gain = lambda k, shape: 1.0 + 0.05 * jax.random.normal(k, shape, f32)

    def lam_init(k):
        u = jax.random.uniform(k, (DEPTH, LRU_WIDTH), f32, minval=0.9, maxval=0.999)
        a = u ** (1.0 / LRU_C)
        return jnp.log(a) - jnp.log1p(-a)

    return {
        "x_prompt": jax.random.normal(ks[0], (BATCH, SEQ, D_MODEL), f32),
        "x_sample": jax.random.normal(ks[1], (DEC_BATCH, DEC_SEQ, D_MODEL), f32),
        "norm_pre_mix": gain(ks[2], (DEPTH, D_MODEL)),
        "w_in": nrm(ks[3], (DEPTH, D_MODEL, IN_WIDTH), D_MODEL ** -0.5),
        "ret_norm": gain(ks[4], (DEPTH, RET_HEADS, RET_HEAD_DIM)),
        "conv_w": nrm(ks[5], (DEPTH, CONV_WIDTH, LRU_WIDTH), CONV_WIDTH ** -0.5),
        "conv_b": nrm(ks[6], (DEPTH, LRU_WIDTH), 0.01),
        "lru_wa_fwd": nrm(ks[7], (DEPTH, LRU_BLOCKS, LRU_BLOCK, LRU_BLOCK), LRU_BLOCK ** -0.5),
        "lru_ba_fwd": nrm(ks[8], (DEPTH, LRU_WIDTH), 0.01),
        "lru_wx_fwd": nrm(ks[9], (DEPTH, LRU_BLOCKS, LRU_BLOCK, LRU_BLOCK), LRU_BLOCK ** -0.5),
        "lru_bx_fwd": nrm(ks[10], (DEPTH, LRU_WIDTH), 0.01),
        "lru_lam_fwd": lam_init(ks[11]),
        "lru_wa_bwd": nrm(ks[12], (DEPTH, LRU_BLOCKS, LRU_BLOCK, LRU_BLOCK), LRU_BLOCK ** -0.5),
        "lru_ba_bwd": nrm(ks[13], (DEPTH, LRU_WIDTH), 0.01),
        "lru_wx_bwd": nrm(ks[14], (DEPTH, LRU_BLOCKS, LRU_BLOCK, LRU_BLOCK), LRU_BLOCK ** -0.5),
        "lru_bx_bwd": nrm(ks[15], (DEPTH, LRU_WIDTH), 0.01),
        "lru_lam_bwd": lam_init(ks[16]),
        "lru_norm": gain(ks[17], (DEPTH, LRU_WIDTH)),
        "w_out": nrm(ks[18], (DEPTH, MIX_WIDTH, D_MODEL), MIX_WIDTH ** -0.5),
        "norm_post_mix": gain(ks[19], (DEPTH, D_MODEL)),
        "norm_pre_ffn": gain(ks[20], (DEPTH, D_MODEL)),
        "w_gate": nrm(ks[21], (DEPTH, D_MODEL, D_FF), D_MODEL ** -0.5),
        "w_up": nrm(ks[22], (DEPTH, D_MODEL, D_FF), D_MODEL ** -0.5),
        "w_down": nrm(ks[23], (DEPTH, D_FF, D_MODEL), D_FF ** -0.5),
        "norm_post_ffn": gain(ks[24], (DEPTH, D_MODEL)),
    }


def reference(x_prompt, x_sample, norm_pre_mix, w_in, ret_norm, conv_w, conv_b,
              lru_wa_fwd, lru_ba_fwd, lru_wx_fwd, lru_bx_fwd, lru_lam_fwd,
              lru_wa_bwd, lru_ba_bwd, lru_wx_bwd, lru_bx_bwd, lru_lam_bwd,
              lru_norm, w_out, norm_post_mix, norm_pre_ffn, w_gate, w_up, w_down, norm_post_ffn):
    params = (norm_pre_mix, w_in, ret_norm, conv_w, conv_b,
              lru_wa_fwd, lru_ba_fwd, lru_wx_fwd, lru_bx_fwd, lru_lam_fwd,
              lru_wa_bwd, lru_ba_bwd, lru_wx_bwd, lru_bx_bwd, lru_lam_bwd,
              lru_norm, w_out, norm_post_mix, norm_pre_ffn, w_gate, w_up, w_down, norm_post_ffn)
    y_prompt = _trunk(x_prompt, params)
    y_sample = _trunk(x_sample, params)
    return (y_prompt, y_sample)
```

```python
import numpy as np
import concourse.bass as bass
import concourse.mybir as mybir
from concourse.bass_utils import run_bass_kernel_spmd

F32 = mybir.dt.float32
BF16 = mybir.dt.bfloat16
ALU = mybir.AluOpType
AF = mybir.ActivationFunctionType

D = 2048
KD = 16
INW = 6144
DFF = 5632
KF = 44
H = 8
TT = 512
EPS = 1e-6
NCORES = 8

ENGS = ("sp", "act", "pe", "dve", "pool")


class Prog:
    def __init__(self):
        self.ops = []
        self.lw = {}
        self.rd = {}
        self.last_eng = {}
        self.last_chan = {}
        self.bar_excl = set()

    def op(self, eng, fn, reads=(), writes=(), chan=None):
        i = len(self.ops)
        deps = {}
        for r in reads:
            w = self.lw.get(r)
            if w is not None:
                deps[w] = "RAW"
        for r in writes:
            w = self.lw.get(r)
            if w is not None and w not in deps:
                deps[w] = "WAW"
            rr = self.rd.get(r)
            if rr:
                for q in rr[0].values():
                    if q not in deps:
                        deps[q] = "WAR"
                for q in rr[1]:
                    if q not in deps:
                        deps[q] = "WAR"
        for r in reads:
            rr = self.rd.get(r)
            if rr is None:
                rr = self.rd[r] = [{}, []]
            if chan is not None:
                rr[1].append(i)
            else:
                rr[0][eng] = i
        for r in writes:
            self.lw[r] = i
            self.rd[r] = [{}, []]
        self.ops.append([eng, fn, deps, chan])
        self.last_eng[eng] = i
        if chan is not None:
            self.last_chan[chan] = i
        return i

    def barrier(self):
        lasts = set(self.last_eng.values())
        for c, i in self.last_chan.items():
            if c not in self.bar_excl:
                lasts.add(i)
        for eng in ENGS:
            i = len(self.ops)
            deps = {w: "BAR" for w in lasts}
            self.ops.append([eng, None, deps, None])
            self.last_eng[eng] = i

    def emit(self, nc, block, sems_ctx):
        ops = self.ops
        waits = [None] * len(ops)
        need_signal = set()
        for i, (eng, fn, deps, chan) in enumerate(ops):
            wl = []
            for w, kind in deps.items():
                weng, _, _, wchan = ops[w]
                if ops[w][1] is None:
                    continue
                if wchan is not None or chan is not None or weng != eng:
                    need = True
                else:
                    need = (eng != "pe") and kind in ("RAW", "BAR")
                if need:
                    wl.append(w)
                    if wchan is None:
                        need_signal.add(w)
            waits[i] = wl
        tick = {}
        cnt = {}
        for i, (eng, fn, deps, chan) in enumerate(ops):
            if fn is None:
                continue
            if chan is not None:
                key = ("ch", chan)
                cnt[key] = cnt.get(key, 0) + 16
                tick[i] = (key, cnt[key])
            elif i in need_signal:
                key = ("eng", eng)
                cnt[key] = cnt.get(key, 0) + 1
                tick[i] = (key, cnt[key])
        semobj = {}
        for key in cnt:
            semobj[key] = sems_ctx(("s_%s_%s" % (key[0], str(key[1]))).replace(" ", "").replace("'", "").replace("(", "").replace(")", "").replace(",", "_"))
        self.n_sems = len(semobj)
        per_eng = {e: [] for e in ENGS}
        for i, o in enumerate(ops):
            per_eng[o[0]].append(i)

        def run(eng_name, e):
            seen = {}
            for i in per_eng[eng_name]:
                _, fn, _, chan = ops[i]
                for w in waits[i]:
                    key, val = tick[w]
                    if seen.get(key, 0) < val:
                        e.wait_ge(semobj[key], val)
                        seen[key] = val
                if fn is not None:
                    ins = fn(e)
                    if i in tick:
                        ins.then_inc(semobj[tick[i][0]], 16 if chan is not None else 1)

        @block.sync
        def _(e):
            run("sp", e)

        @block.scalar
        def _(e):
            run("act", e)

        @block.tensor
        def _(e):
            run("pe", e)

        @block.vector
        def _(e):
            run("dve", e)

        @block.gpsimd
        def _(e):
            run("pool", e)


class Arena:
    def __init__(self, tensor, nelem):
        self.t = tensor
        self.n = nelem
        self.base = 0
        self.cur = 0

    def mark(self):
        self.base = self.cur

    def reset(self):
        self.cur = self.base

    def _take(self, nbytes):
        nb = (nbytes + 63) // 64 * 64
        off = self.cur
        self.cur += nb // 2
        assert self.cur <= self.n, "arena overflow: %d > %d (bytes/partition)" % (self.cur * 2, self.n * 2)
        return off

    def bf(self, shape):
        n = int(np.prod(shape))
        off = self._take(n * 2)
        v = self.t[:, off:off + n]
        return self._shape(v, shape)

    def f32(self, shape):
        n = int(np.prod(shape))
        off = self._take(n * 4)
        v = self.t[:, off:off + 2 * n].bitcast(F32)
        return self._shape(v, shape)

    @staticmethod
    def _shape(v, shape):
        if len(shape) == 1:
            return v
        if len(shape) == 2:
            return v.rearrange("p (a b) -> p a b", b=shape[1])
        if len(shape) == 3:
            return v.rearrange("p (a b c) -> p a b c", b=shape[1], c=shape[2])
        raise ValueError(shape)


def _pcol(v, nchunk):
    v = np.asarray(v, np.float32)
    lead = v.shape[:-1]
    v = v.reshape(lead + (nchunk, 128))
    return np.moveaxis(v, -1, 0)


class ParamIdx:
    def __init__(self, L):
        self.L = L
        o = 0
        self.off = {}
        for name, n in (("npm", 16), ("npo", 16), ("npf", 16), ("npff", 16), ("retn", 8), ("lrun", 8),
                        ("cw0", 8), ("cw1", 8), ("cw2", 8), ("cw3", 8), ("cb", 8),
                        ("baf", 8), ("bxf", 8), ("bab", 8), ("bxb", 8), ("lamf", 8), ("lamb", 8)):
            self.off[name] = o
            self.w = n
            o += n * L
        self.n = o
        self.width = {k: (16 if k in ("npm", "npo", "npf", "npff") else 8) for k in self.off}

    def idx(self, name, l, c):
        return self.off[name] + l * self.width[name] + c


def _pack_params(inp, L):
    pi = ParamIdx(L)
    out = np.zeros((128, pi.n), np.float32)

    def put(name, arr, nchunk):
        a = _pcol(arr, nchunk)
        o = pi.off[name]
        out[:, o:o + L * nchunk] = a.reshape(128, L * nchunk)

    put("npm", inp["norm_pre_mix"][:L], 16)
    put("npo", inp["norm_post_mix"][:L], 16)
    put("npf", inp["norm_pre_ffn"][:L], 16)
    put("npff", inp["norm_post_ffn"][:L], 16)
    put("retn", np.asarray(inp["ret_norm"])[:L].reshape(L, 1024), 8)
    put("lrun", inp["lru_norm"][:L], 8)
    cw = np.asarray(inp["conv_w"])[:L]
    for t in range(4):
        put("cw%d" % t, cw[:, t], 8)
    put("cb", inp["conv_b"][:L], 8)
    put("baf", inp["lru_ba_fwd"][:L], 8)
    put("bxf", inp["lru_bx_fwd"][:L], 8)
    put("bab", inp["lru_ba_bwd"][:L], 8)
    put("bxb", inp["lru_bx_bwd"][:L], 8)
    put("lamf", inp["lru_lam_fwd"][:L], 8)
    put("lamb", inp["lru_lam_bwd"][:L], 8)
    return out


def _host_consts(S):
    T = 2 * S
    f32 = np.float32
    inv = (f32(10000.0) ** (-(np.arange(0, 128, 2, dtype=f32)) / f32(128))).astype(f32)
    ropes = []
    for linked in (True, False):
        pos = np.arange(T, dtype=f32) if linked else np.concatenate([np.arange(S, dtype=f32)] * 2)
        ang = (pos[:, None] * inv[None, :]).astype(f32)
        cos = np.cos(ang).astype(f32).T
        sin = np.sin(ang).astype(f32).T
        cosT = np.concatenate([cos, cos], 0)
        sinS = np.concatenate([sin, -sin], 0)
        ropes.append(np.stack([cosT, sinS], 0).astype(f32))
    log_g = np.log1p(-(f32(2.0) ** (-5.0 - np.arange(8, dtype=f32)))).astype(f32)
    pos = np.arange(128, dtype=f32)
    dmat = np.exp(log_g[:, None, None] * np.abs(pos[:, None] - pos[None, :])[None]).astype(f32)
    kf = np.exp(log_g[None, :] * (127.0 - pos)[:, None]).astype(f32)
    kb = np.exp(log_g[None, :] * pos[:, None]).astype(f32)
    qf = np.exp(log_g[None, :] * (pos + 1.0)[:, None]).astype(f32)
    qb = np.exp(log_g[None, :] * (128.0 - pos)[:, None]).astype(f32)
    cst = np.zeros((5, 128, 8, 128), f32)
    cst[0] = dmat.transpose(1, 0, 2)
    cst[1] = np.broadcast_to(kf[:, :, None], (128, 8, 128))
    cst[2] = np.broadcast_to(kb[:, :, None], (128, 8, 128))
    cst[3] = np.broadcast_to(qf.T[None, :, :], (128, 8, 128))
    cst[4] = np.broadcast_to(qb.T[None, :, :], (128, 8, 128))
    gch = [float(np.exp(log_g[h] * f32(128.0)).astype(f32)) for h in range(8)]
    return ropes, cst.reshape(5, 128, 1024), gch


def build_program(L, S, dbg=(), stop_after=None):
    T = 2 * S
    NT = T // TT
    HT = NT // 2
    NCH = T // 128
    pi = ParamIdx(L)
    _, _, GCH = _host_consts(128)

    nc = bass.Bass("TRN2", target_bir_lowering=False)
    outs = []

    def din(name, shape, dt=F32):
        return nc.dram_tensor(name, list(shape), dt, kind="ExternalInput").ap()

    def dscr(name, shape, dt):
        if name in dbg:
            outs.append(name)
            return nc.dram_tensor(name, list(shape), dt, kind="ExternalOutput").ap()
        return nc.dram_tensor(name, list(shape), dt, kind="Internal").ap()

    x_in = din("x", [T, D])
    y_out = nc.dram_tensor("y", [T, D], F32, kind="ExternalOutput").ap()
    w_in = din("w_in", [L, D, INW])
    w_out = din("w_out", [L, D, D])
    w_gate = din("w_gate", [L, D, DFF])
    w_up = din("w_up", [L, D, DFF])
    w_down = din("w_down", [L, DFF, D])
    lruw = din("lruw", [L, 4, 8, 128, 128])
    params_d = din("params", [128, pi.n])
    cst_d = din("cst", [5, 128, 1024])
    rope_d = din("rope", [2, 128, T])
    ident_d = din("ident", [128, 128])
    link_d = din("link", [128, 1])

    wb_in = dscr("wb_in", [L, 12, 128, 16, 512], BF16)
    wb_out = dscr("wb_out", [L, 8, 128, 16, 256], BF16)
    wb_gate = dscr("wb_gate", [L, 22, 128, 16, 256], BF16)
    wb_up = dscr("wb_up", [L, 22, 128, 16, 256], BF16)
    wb_down = dscr("wb_down", [L, 16, 128, 44, 128], BF16)
    xs = dscr("xs", [NT, 128, 16, 512], F32)
    q_s = dscr("q_s", [NT, 128, 8, 512], BF16)
    k_s = dscr("k_s", [NT, 128, 8, 512], BF16)
    g_s = dscr("g_s", [NT, 128, 8, 512], BF16)
    gy_s = dscr("gy_s", [NT, 128, 8, 512], BF16)
    v_s = dscr("v_s", [NCH, 128, 1024], BF16)
    ux_s = dscr("ux_s", [128, 8, T], F32)
    yp_s = dscr("yp_s", [NT, 128, 8, 512], F32)
    hf_s = dscr("hf_s", [NT, 128, 8, 512], F32)
    ab_s = dscr("ab_s", [NT, 128, 8, 512], F32)
    bb_s = dscr("bb_s", [NT, 128, 8, 512], F32)
    mix_s = dscr("mix_s", [NT, 128, 16, 512], BF16)

    P = Prog()
    ARENA_BYTES = 204 * 1024

    import contextlib
    es = contextlib.ExitStack()
    with es:
        arena_t = es.enter_context(nc.sbuf_tensor("arena", [128, ARENA_BYTES // 2], BF16))
        psum_t = es.enter_context(nc.psum_tensor("psum", [128, 8, 512], F32))
        A = Arena(arena_t, ARENA_BYTES // 2)

        def sems_ctx(name):
            return es.enter_context(nc.semaphore(name))

        def PS(b, n=1):
            return psum_t[:, b:b + n, :].rearrange("p a b -> p (a b)")

        identF = A.f32([128])
        identB = A.bf([128])
        onesB = A.bf([128])
        params = A.f32([pi.n])
        linkm = A.f32([1])
        epsT = A.f32([1])
        oneT = A.f32([1])
        c1t = A.f32([2 * L * 8])
        c2t = A.f32([2 * L * 8])
        A.mark()

        def par(name, l, c):
            j = pi.idx(name, l, c)
            return params[:, j:j + 1]

        P.op("sp", lambda e: e.dma_start(out=identF, in_=ident_d), writes=["identF"], chan="c0a")
        P.op("sp", lambda e: e.dma_start(out=params, in_=params_d), writes=["params"], chan="c0b")
        P.op("sp", lambda e: e.dma_start(out=linkm, in_=link_d), writes=["linkm"], chan="c0c")
        P.op("act", lambda e: e.copy(out=identB, in_=identF), reads=["identF"], writes=["identB"])
        P.op("dve", lambda e: e.memset(onesB, 1.0), writes=["onesB"])
        P.op("dve", lambda e: e.memset(epsT, EPS), writes=["epsT"])
        P.op("dve", lambda e: e.memset(oneT, 1.0), writes=["oneT"])
        lo = pi.off["lamf"]
        nl = 2 * L * 8
        P.op("act", lambda e: e.activation(out=c1t, in_=params[:, lo:lo + nl], func=AF.Exp, scale=-1.0),
             reads=["params"], writes=["c1t"])
        P.op("act", lambda e: e.activation(out=c1t, in_=c1t, func=AF.Ln, bias=oneT[:, 0:1], scale=1.0),
             reads=["c1t", "oneT"], writes=["c1t"])
        P.op("dve", lambda e: e.tensor_scalar(out=c1t, in0=c1t, scalar1=-8.0, scalar2=None, op0=ALU.mult),
             reads=["c1t"], writes=["c1t"])
        P.op("dve", lambda e: e.tensor_scalar(out=c2t, in0=c1t, scalar1=2.0, scalar2=None, op0=ALU.mult),
             reads=["c1t"], writes=["c2t"])

        def convert_layer(l):
            def cv(kind, dst, src, last):
                ch = ("wcv", l, kind)
                P.bar_excl.add(ch)
                P.op("pool", lambda e: e.dma_start(out=dst, in_=src),
                     writes=[("wb", kind, l) if last else ("wbp", kind, l, id(dst))], chan=ch)
            for s in range(12):
                cv("in", wb_in[l, s], w_in[l, :, s * 512:(s + 1) * 512].rearrange("(kc p) c -> p kc c", p=128), s == 11)
            for s in range(8):
                cv("out", wb_out[l, s], w_out[l, :, s * 256:(s + 1) * 256].rearrange("(kc p) c -> p kc c", p=128), s == 7)
            for s in range(22):
                cv("gate", wb_gate[l, s], w_gate[l, :, s * 256:(s + 1) * 256].rearrange("(kc p) c -> p kc c", p=128), s == 21)
                cv("up", wb_up[l, s], w_up[l, :, s * 256:(s + 1) * 256].rearrange("(kc p) c -> p kc c", p=128), s == 21)
            for s in range(16):
                cv("down", wb_down[l, s], w_down[l, :, s * 128:(s + 1) * 128].rearrange("(kc p) c -> p kc c", p=128), s == 15)

        convert_layer(0)

        def phase_tin():
            A.reset()
            xin = [A.f32([4, 2048]) for _ in range(2)]
            xt = [A.f32([16, 512]) for _ in range(2)]
            for t in range(NT):
                sl = t % 2
                src = x_in[t * 512:(t + 1) * 512, :].rearrange("(b p) d -> p b d", p=128)
                P.op("sp", lambda e, sl=sl, src=src: e.dma_start(out=xin[sl], in_=src),
                     writes=[("xin", sl)], chan=("xin", sl))
                for kc in range(16):
                    bank = kc % 4
                    def tr(e, sl=sl, kc=kc, bank=bank):
                        for b in range(4):
                            ins = e.transpose(out=PS(bank)[:, b * 128:(b + 1) * 128],
                                              in_=xin[sl][:, b, kc * 128:(kc + 1) * 128], identity=identF)
                        return ins
                    P.op("pe", tr, reads=[("xin", sl), "identF"], writes=[("ps", bank)])
                    eng = "act" if kc % 2 == 0 else "dve"
                    if eng == "act":
                        fn = lambda e, sl=sl, kc=kc, bank=bank: e.copy(out=xt[sl][:, kc, :], in_=PS(bank))
                    else:
                        fn = lambda e, sl=sl, kc=kc, bank=bank: e.tensor_copy(out=xt[sl][:, kc, :], in_=PS(bank))
                    P.op(eng, fn, reads=[("ps", bank)], writes=[("xt", sl, kc)])
                P.op("sp", lambda e, sl=sl, t=t: e.dma_start(out=xs[t], in_=xt[sl]),
                     reads=[("xt", sl, kc) for kc in range(16)], chan=("xst", sl))
            P.barrier()

        def phase_a(l):
            A.reset()
            xA = [A.f32([16, 512]) for _ in range(2)]
            sqb = A.bf([16, 512])
            hT = [A.bf([16, 512]) for _ in range(2)]
            rs = A.f32([512])
            rstd = A.f32([512])
            ropeC = [A.f32([512]) for _ in range(2)]
            ropeS = [A.f32([512]) for _ in range(2)]
            NWR = 3
            wr = [A.bf([16, 512]) for _ in range(NWR)]
            NSTG = 6
            stgb = [A.bf([512]) for _ in range(NSTG)]
            stgf = [A.f32([512]) for _ in range(4)]
            tmp1 = [A.f32([512]) for _ in range(3)]
            tmp2 = [A.f32([512]) for _ in range(3)]
            cnt = {"wr": 0, "ps": 0, "sb": 0, "sf": 0, "tm": 0}
            NPSB = 6
            SC = float(128.0 ** -0.5)

            def load_x(t):
                sl = t % 2
                P.op("sp", lambda e: e.dma_start(out=xA[sl], in_=xs[t]), writes=[("xA", sl)], chan=("xA", sl))
                P.op("sp", lambda e: e.dma_start(out=ropeC[sl], in_=rope_d[0, :, t * 512:(t + 1) * 512]),
                     writes=[("ropeC", sl)], chan=("rope", sl))
                P.op("sp", lambda e: e.dma_start(out=ropeS[sl], in_=rope_d[1, :, t * 512:(t + 1) * 512]),
                     writes=[("ropeS", sl)], chan=("ropeS", sl))

            def load_w(s):
                slot = cnt["wr"] % NWR
                cnt["wr"] += 1
                P.op("sp", lambda e: e.dma_start(out=wr[slot], in_=wb_in[l, s]),
                     reads=[("wb", "in", l)], writes=[("wr", slot)], chan=("wr", slot))
                return slot

            load_x(0)

            def do_tile(t):
                sl = t % 2
                if t + 1 < NT:
                    load_x(t + 1)
                wslots = {}
                wslots[0] = load_w(0)
                wslots[1] = load_w(1)
                P.op("act", lambda e: e.activation(out=sqb, in_=xA[sl], func=AF.Square),
                     reads=[("xA", sl)], writes=["sqb"])

                def ssq(e):
                    for kc in range(16):
                        ins = e.matmul(PS(7), lhsT=onesB, rhs=sqb[:, kc, :], start=(kc == 0), stop=(kc == 15))
                    return ins
                P.op("pe", ssq, reads=["sqb", "onesB"], writes=[("ps", 7)])
                P.op("act", lambda e: e.activation(out=rs, in_=PS(7), func=AF.Sqrt, bias=epsT[:, 0:1], scale=1.0 / D),
                     reads=[("ps", 7), "epsT"], writes=["rs"])
                P.op("dve", lambda e: e.reciprocal(out=rstd, in_=rs), reads=["rs"], writes=["rstd"])
                for kc in range(16):
                    P.op("dve", lambda e, kc=kc: e.scalar_tensor_tensor(
                        out=hT[sl][:, kc, :], in0=xA[sl][:, kc, :], scalar=par("npm", l, kc), in1=rstd,
                        op0=ALU.mult, op1=ALU.mult),
                        reads=[("xA", sl), "rstd", "params"], writes=[("hT", sl, kc)])
                hreads = [("hT", sl, kc) for kc in range(16)]
                for s in range(12):
                    if s + 2 < 12:
                        wslots[s + 2] = load_w(s + 2)
                    ws = wslots[s]
                    kind = s // 2
                    if kind == 2:
                        for b in range(4):
                            bank = cnt["ps"] % NPSB
                            cnt["ps"] += 1

                            def mm(e, b=b, bank=bank, ws=ws):
                                for kc in range(16):
                                    ins = e.matmul(PS(bank), lhsT=hT[sl][:, kc, b * 128:(b + 1) * 128],
                                                   rhs=wr[ws][:, kc, :], start=(kc == 0), stop=(kc == 15))
                                return ins
                            P.op("pe", mm, reads=hreads + [("wr", ws)], writes=[("ps", bank)])
                            sb = cnt["sb"] % NSTG
                            cnt["sb"] += 1
                            P.op("act", lambda e, bank=bank, sb=sb: e.copy(out=stgb[sb], in_=PS(bank)),
                                 reads=[("ps", bank)], writes=[("stgb", sb)])
                            ch = t * 4 + b
                            c0 = (s - 4) * 512
                            P.op("sp", lambda e, sb=sb, ch=ch, c0=c0: e.dma_start(out=v_s[ch, :, c0:c0 + 512], in_=stgb[sb]),
                                 reads=[("stgb", sb)], chan=("stgb", sb))
                        continue
                    for m in range(4):
                        mt = (s % 2) * 4 + m
                        bank = cnt["ps"] % NPSB
                        cnt["ps"] += 1

                        def mm(e, m=m, bank=bank, ws=ws):
                            for kc in range(16):
                                ins = e.matmul(PS(bank), lhsT=wr[ws][:, kc, m * 128:(m + 1) * 128],
                                               rhs=hT[sl][:, kc, :], start=(kc == 0), stop=(kc == 15))
                            return ins
                        P.op("pe", mm, reads=hreads + [("wr", ws)], writes=[("ps", bank)])
                        if kind in (0, 1):
                            sc = 1.0 if kind == 0 else SC
                            ti = cnt["tm"] % 3
                            cnt["tm"] += 1
                            P.op("act", lambda e, bank=bank, ti=ti, sc=sc: e.activation(
                                out=tmp1[ti], in_=PS(bank), func=AF.Copy, scale=sc),
                                reads=[("ps", bank)], writes=[("tmp1", ti)])
                            P.op("dve", lambda e, ti=ti: e.tensor_tensor(
                                out=tmp2[ti][0:64, :], in0=tmp1[ti][64:128, :], in1=ropeS[sl][64:128, :], op=ALU.mult),
                                reads=[("tmp1", ti), ("ropeS", sl)], writes=[("tmp2", ti, 0)])
                            P.op("dve", lambda e, ti=ti: e.tensor_tensor(
                                out=tmp2[ti][64:128, :], in0=tmp1[ti][0:64, :], in1=ropeS[sl][0:64, :], op=ALU.mult),
                                reads=[("tmp1", ti), ("ropeS", sl)], writes=[("tmp2", ti, 1)])
                            P.op("pool", lambda e, ti=ti: e.tensor_tensor(
                                out=tmp1[ti], in0=tmp1[ti], in1=ropeC[sl], op=ALU.mult),
                                reads=[("tmp1", ti), ("ropeC", sl), ("tmp2", ti, 0), ("tmp2", ti, 1)], writes=[("tmp1", ti)])
                            sb = cnt["sb"] % NSTG
                            cnt["sb"] += 1
                            P.op("pool", lambda e, ti=ti, sb=sb: e.tensor_tensor(
                                out=stgb[sb], in0=tmp1[ti], in1=tmp2[ti], op=ALU.add),
                                reads=[("tmp1", ti), ("tmp2", ti, 0), ("tmp2", ti, 1)], writes=[("stgb", sb)])
                            dst = (q_s if kind == 0 else k_s)[t, :, mt, :]
                            P.op("sp", lambda e, sb=sb, dst=dst: e.dma_start(out=dst, in_=stgb[sb]),
                                 reads=[("stgb", sb)], chan=("stgb", sb))
                        elif kind in (3, 5):
                            fnc = AF.Silu if kind == 3 else AF.Gelu_apprx_tanh
                            sb = cnt["sb"] % NSTG
                            cnt["sb"] += 1
                            P.op("act", lambda e, bank=bank, sb=sb, fnc=fnc: e.activation(
                                out=stgb[sb], in_=PS(bank), func=fnc),
                                reads=[("ps", bank)], writes=[("stgb", sb)])
                            dst = (g_s if kind == 3 else gy_s)[t, :, mt, :]
                            P.op("sp", lambda e, sb=sb, dst=dst: e.dma_start(out=dst, in_=stgb[sb]),
                                 reads=[("stgb", sb)], chan=("stgb", sb))
                        else:
                            sf = cnt["sf"] % 4
                            cnt["sf"] += 1
                            P.op("dve", lambda e, bank=bank, sf=sf: e.tensor_copy(out=stgf[sf], in_=PS(bank)),
                                 reads=[("ps", bank)], writes=[("stgf", sf)])
                            dst = ux_s[:, mt, t * 512:(t + 1) * 512]
                            P.op("sp", lambda e, sf=sf, dst=dst: e.dma_start(out=dst, in_=stgf[sf]),
                                 reads=[("stgf", sf)], chan=("stgf", sf))
            for t in range(NT):
                do_tile(t)
            P.barrier()


        def phase_b1(l):
            A.reset()
            Dtab = A.f32([8, 128])
            kftab = A.f32([8, 128])
            qftab = A.f32([8, 128])
            gw = A.bf([4, 8, 128])
            qT = A.bf([8, 512])
            kT = A.bf([8, 512])
            vt = A.bf([4, 1024])
            yp = A.f32([8, 512])
            kf = A.bf([8, 128])
            Pm = A.bf([8, 128])
            qf = A.bf([8, 128])
            sf = A.f32([8, 128])
            sfb = A.bf([8, 128])
            uxw = A.f32([8, 515])
            uc = A.f32([8, 512])
            ucb = A.bf([8, 512])
            rr = A.f32([8, 512])
            ig = A.f32([8, 512])
            aa = A.f32([8, 512])
            hf = A.f32([8, 512])
            carry = A.f32([8])
            PST = PS(0).bitcast(BF16)
            PSS = PS(1, 2).rearrange("p (h i) -> p h i", i=128)
            PSY = PS(3, 2).rearrange("p (h i) -> p h i", i=128)
            fl = lambda ap: ap.rearrange("p h i -> p (h i)")

            P.op("sp", lambda e: e.dma_start(out=fl(Dtab), in_=cst_d[0]), writes=["Dtab"], chan="tb0")
            P.op("sp", lambda e: e.dma_start(out=fl(kftab), in_=cst_d[1]), writes=["kftab"], chan="tb1")
            P.op("sp", lambda e: e.dma_start(out=fl(qftab), in_=cst_d[3]), writes=["qftab"], chan="tb2")
            for k4 in range(4):
                P.op("pool", lambda e, k4=k4: e.dma_start(out=gw[:, k4], in_=lruw[l, k4].rearrange("g i j -> i g j")),
                     writes=[("gw", k4)], chan="gw")
            P.op("dve", lambda e: e.memset(fl(sf), 0.0), writes=["sf"])
            P.op("dve", lambda e: e.memset(fl(sfb), 0.0), writes=["sfb"])
            P.op("dve", lambda e: e.memset(carry, 0.0), writes=["carry"])
            gwr = [("gw", k4) for k4 in range(4)]

            def do_tile(t):
                P.op("sp", lambda e: e.dma_start(out=qT, in_=q_s[t]), writes=["qT"], chan="ldq")
                P.op("sp", lambda e: e.dma_start(out=kT, in_=k_s[t]), writes=["kT"], chan="ldk")
                P.op("sp", lambda e: e.dma_start(out=vt, in_=v_s[t * 4:(t + 1) * 4].rearrange("c p e -> p c e")),
                     writes=["vt"], chan="ldv")
                lo = t * 512 - 2
                hi = t * 512 + 513
                wl, wh = 0, 515
                if lo < 0:
                    wl, lo = 2, 0
                    P.op("dve", lambda e: e.memset(uxw[:, :, 0:2], 0.0), writes=["uxw"])
                if hi > T:
                    wh, hi = 514, T
                    P.op("dve", lambda e: e.memset(uxw[:, :, 514:515], 0.0), writes=["uxw"])
                P.op("sp", lambda e: e.dma_start(out=uxw[:, :, wl:wh], in_=ux_s[:, :, lo:hi]), writes=["uxw"], chan="ldu")
                if t == HT:
                    P.op("dve", lambda e: e.tensor_scalar(out=uxw[:, :, 0:2], in0=uxw[:, :, 0:2], scalar1=linkm[:, 0:1],
                                                          scalar2=None, op0=ALU.mult), reads=["uxw", "linkm"], writes=["uxw"])
                if t == HT - 1:
                    P.op("dve", lambda e: e.tensor_scalar(out=uxw[:, :, 514:515], in0=uxw[:, :, 514:515], scalar1=linkm[:, 0:1],
                                                          scalar2=None, op0=ALU.mult), reads=["uxw", "linkm"], writes=["uxw"])
                for c in range(4):
                    n = t * 4 + c
                    cs = slice(c * 128, (c + 1) * 128)

                    def trk(e, cs=cs):
                        for h in range(8):
                            ins = e.transpose(out=PST[:, h * 128:(h + 1) * 128], in_=kT[:, h, cs], identity=identB)
                        return ins
                    P.op("pe", trk, reads=["kT", "identB"], writes=[("ps", 0)])
                    P.op("dve", lambda e: e.tensor_tensor(out=fl(kf), in0=PST, in1=fl(kftab), op=ALU.mult),
                         reads=[("ps", 0), "kftab"], writes=["kf"])

                    def sc(e, cs=cs):
                        for h in range(8):
                            ins = e.matmul(PSS[:, h, :], lhsT=kT[:, h, cs], rhs=qT[:, h, cs], start=True, stop=True)
                        return ins
                    P.op("pe", sc, reads=["kT", "qT"], writes=[("ps", 1), ("ps", 2)])
                    P.op("dve", lambda e: e.tensor_tensor(out=fl(Pm), in0=fl(PSS), in1=fl(Dtab), op=ALU.mult),
                         reads=[("ps", 1), ("ps", 2), "Dtab"], writes=["Pm"])
                    P.op("pool", lambda e, cs=cs: e.tensor_tensor(out=qf, in0=qT[:, :, cs], in1=qftab, op=ALU.mult),
                         reads=["qT", "qftab"], writes=["qf"])

                    def ymm(e, c=c):
                        for h in range(8):
                            e.matmul(PSY[:, h, :], lhsT=vt[:, c, h * 128:(h + 1) * 128], rhs=Pm[:, h, :], start=True, stop=False)
                            ins = e.matmul(PSY[:, h, :], lhsT=sfb[:, h, :], rhs=qf[:, h, :], start=False, stop=True)
                        return ins
                    P.op("pe", ymm, reads=["vt", "Pm", "sfb", "qf"], writes=[("ps", 3), ("ps", 4)])
                    P.op("act", lambda e, cs=cs: e.copy(out=yp[:, :, cs], in_=PSY),
                         reads=[("ps", 3), ("ps", 4)], writes=[("yp", c)])

                    def kvm(e, c=c):
                        for h in range(8):
                            ins = e.matmul(PSS[:, h, :], lhsT=kf[:, h, :], rhs=vt[:, c, h * 128:(h + 1) * 128], start=True, stop=True)
                        return ins
                    P.op("pe", kvm, reads=["kf", "vt"], writes=[("ps", 1), ("ps", 2)])

                    def upd(e):
                        for h in range(8):
                            ins = e.scalar_tensor_tensor(out=sf[:, h, :], in0=sf[:, h, :], scalar=GCH[h], in1=PSS[:, h, :],
                                                         op0=ALU.mult, op1=ALU.add)
                        return ins
                    P.op("dve", upd, reads=["sf", ("ps", 1), ("ps", 2)], writes=["sf"])
                    if n == NCH // 2 - 1:
                        P.op("dve", lambda e: e.tensor_scalar(out=fl(sf), in0=fl(sf), scalar1=linkm[:, 0:1], scalar2=None,
                                                              op0=ALU.mult), reads=["sf", "linkm"], writes=["sf"])
                    P.op("act", lambda e: e.copy(out=fl(sfb), in_=fl(sf)), reads=["sf"], writes=["sfb"])
                P.op("sp", lambda e: e.dma_start(out=yp_s[t], in_=yp), reads=[("yp", c) for c in range(4)], chan="sty")
                for blk in range(8):
                    P.op("act", lambda e, blk=blk: e.activation(out=uc[:, blk, :], in_=uxw[:, blk, 0:512], func=AF.Identity,
                                                                bias=par("cb", l, blk), scale=par("cw0", l, blk)),
                         reads=["uxw", "params"], writes=[("uc", blk)])
                    for tap in (1, 2, 3):
                        P.op("dve", lambda e, blk=blk, tap=tap: e.scalar_tensor_tensor(
                            out=uc[:, blk, :], in0=uxw[:, blk, tap:tap + 512], scalar=par("cw%d" % tap, l, blk),
                            in1=uc[:, blk, :], op0=ALU.mult, op1=ALU.add),
                            reads=["uxw", "params", ("uc", blk)], writes=[("uc", blk)])
                ucr = [("uc", blk) for blk in range(8)]
                P.op("act", lambda e: e.copy(out=ucb, in_=uc), reads=ucr, writes=["ucb"])
                gcnt = [0]
                for d in range(2):
                    for k2 in range(2):
                        kind = d * 2 + k2
                        dest = rr if k2 == 0 else ig
                        dname = "rr" if k2 == 0 else "ig"
                        bname = ("baf", "bxf", "bab", "bxb")[kind]
                        for blk in range(8):
                            bank = 5 + gcnt[0] % 3
                            gcnt[0] += 1
                            P.op("pe", lambda e, kind=kind, blk=blk, bank=bank: e.matmul(
                                PS(bank), lhsT=gw[:, kind, blk, :], rhs=ucb[:, blk, :], start=True, stop=True),
                                reads=["ucb"] + gwr, writes=[("ps", bank)])
                            P.op("act", lambda e, dest=dest, blk=blk, bank=bank, bname=bname: e.activation(
                                out=dest[:, blk, :], in_=PS(bank), func=AF.Sigmoid, bias=par(bname, l, blk), scale=1.0),
                                reads=[("ps", bank), "params"], writes=[(dname, blk)])
                    for blk in range(8):
                        j = d * L * 8 + l * 8 + blk
                        P.op("act", lambda e, blk=blk, j=j: e.activation(out=aa[:, blk, :], in_=rr[:, blk, :], func=AF.Exp,
                                                                         scale=c1t[:, j:j + 1]),
                             reads=[("rr", blk), "c1t"], writes=[("aa", blk)])
                        P.op("act", lambda e, blk=blk, j=j: e.activation(out=rr[:, blk, :], in_=rr[:, blk, :], func=AF.Exp,
                                                                         scale=c2t[:, j:j + 1]),
                             reads=[("rr", blk), "c2t"], writes=[("rr", blk)])
                    rrr = [("rr", blk) for blk in range(8)]
                    igr = [("ig", blk) for blk in range(8)]
                    aar = [("aa", blk) for blk in range(8)]
                    P.op("act", lambda e: e.activation(out=rr, in_=rr, func=AF.Sqrt, bias=oneT[:, 0:1], scale=-1.0),
                         reads=rrr + ["oneT"], writes=rrr)
                    P.op("pool", lambda e: e.tensor_tensor(out=ig, in0=ig, in1=uc, op=ALU.mult), reads=igr + ucr, writes=igr)
                    P.op("pool", lambda e: e.tensor_tensor(out=ig, in0=ig, in1=rr, op=ALU.mult), reads=igr + rrr, writes=igr)
                    if d == 0:
                        def scan(e):
                            for blk in range(8):
                                ins = e.tensor_tensor_scan(out=hf[:, blk, :], data0=aa[:, blk, :], data1=ig[:, blk, :],
                                                           initial=carry[:, blk:blk + 1], op0=ALU.mult, op1=ALU.add)
                            return ins
                        P.op("dve", scan, reads=aar + igr + ["carry"], writes=["hf"])
                        P.op("dve", lambda e: e.tensor_copy(out=carry, in_=hf[:, :, 511]), reads=["hf"], writes=["carry"])
                        if t == HT - 1:
                            P.op("dve", lambda e: e.tensor_scalar(out=carry, in0=carry, scalar1=linkm[:, 0:1], scalar2=None,
                                                                  op0=ALU.mult), reads=["carry", "linkm"], writes=["carry"])
                        P.op("sp", lambda e: e.dma_start(out=hf_s[t], in_=hf), reads=["hf"], chan="sth")
                    else:
                        P.op("sp", lambda e: e.dma_start(out=ab_s[t], in_=aa), reads=aar, chan="sta")
                        P.op("sp", lambda e: e.dma_start(out=bb_s[t], in_=ig), reads=igr, chan="stb")

            for t in range(NT):
                do_tile(t)
            P.barrier()

        def phase_b2(l):
            A.reset()
            kbtab = A.f32([8, 128])
            qbtab = A.f32([8, 128])
            qT = A.bf([8, 512])
            kT = A.bf([8, 512])
            vt = A.bf([4, 1024])
            yp = A.f32([8, 512])
            sg = A.bf([8, 512])
            gy = A.bf([8, 512])
            ab = A.f32([8, 512])
            bb = A.f32([8, 512])
            hf = A.f32([8, 512])
            kb = A.bf([8, 128])
            qb = A.bf([8, 128])
            sb = A.f32([8, 128])
            sbb = A.bf([8, 128])
            ysq = A.bf([8, 512])
            mixb = A.bf([16, 512])
            rsn = [A.f32([512]) for _ in range(2)]
            rstd = [A.f32([512]) for _ in range(2)]
            tmpn = [A.f32([512]) for _ in range(2)]
            carry = A.f32([8])
            PST = PS(0).bitcast(BF16)
            PSS = PS(1, 2).rearrange("p (h i) -> p h i", i=128)
            PSY = PS(3, 2).rearrange("p (h i) -> p h i", i=128)
            fl = lambda ap: ap.rearrange("p h i -> p (h i)")
            rev = lambda ap: bass.AP(ap.tensor, ap.offset + (ap.ap[-1][1] - 1) * ap.ap[-1][0],
                                     [list(x) for x in ap.ap[:-1]] + [[-ap.ap[-1][0], ap.ap[-1][1]]])

            P.op("sp", lambda e: e.dma_start(out=fl(kbtab), in_=cst_d[2]), writes=["kbtab"], chan="tb0")
            P.op("sp", lambda e: e.dma_start(out=fl(qbtab), in_=cst_d[4]), writes=["qbtab"], chan="tb1")
            P.op("dve", lambda e: e.memset(fl(sb), 0.0), writes=["sb"])
            P.op("dve", lambda e: e.memset(fl(sbb), 0.0), writes=["sbb"])
            P.op("dve", lambda e: e.memset(carry, 0.0), writes=["carry"])
            ncnt = [0]

            def do_tile(t):
                P.op("sp", lambda e: e.dma_start(out=qT, in_=q_s[t]), writes=["qT"], chan="ldq")
                P.op("sp", lambda e: e.dma_start(out=kT, in_=k_s[t]), writes=["kT"], chan="ldk")
                P.op("sp", lambda e: e.dma_start(out=vt, in_=v_s[t * 4:(t + 1) * 4].rearrange("c p e -> p c e")),
                     writes=["vt"], chan="ldv")
                P.op("sp", lambda e: e.dma_start(out=yp, in_=yp_s[t]), writes=[("yp", c) for c in range(4)], chan="ldy")
                P.op("sp", lambda e: e.dma_start(out=sg, in_=g_s[t]), writes=["sg"], chan="ldg")
                P.op("sp", lambda e: e.dma_start(out=gy, in_=gy_s[t]), writes=["gy"], chan="ldgy")
                P.op("sp", lambda e: e.dma_start(out=ab, in_=ab_s[t]), writes=["ab"], chan="lda")
                P.op("sp", lambda e: e.dma_start(out=bb, in_=bb_s[t]), writes=["bb"], chan="ldb")
                P.op("sp", lambda e: e.dma_start(out=hf, in_=hf_s[t]), writes=["hf"], chan="ldh")
                for c in (3, 2, 1, 0):
                    n = t * 4 + c
                    cs = slice(c * 128, (c + 1) * 128)

                    def trk(e, cs=cs):
                        for h in range(8):
                            ins = e.transpose(out=PST[:, h * 128:(h + 1) * 128], in_=kT[:, h, cs], identity=identB)
                        return ins
                    P.op("pe", trk, reads=["kT", "identB"], writes=[("ps", 0)])
                    P.op("dve", lambda e: e.tensor_tensor(out=fl(kb), in0=PST, in1=fl(kbtab), op=ALU.mult),
                         reads=[("ps", 0), "kbtab"], writes=["kb"])
                    P.op("pool", lambda e, cs=cs: e.tensor_tensor(out=qb, in0=qT[:, :, cs], in1=qbtab, op=ALU.mult),
                         reads=["qT", "qbtab"], writes=["qb"])

                    def ymm(e):
                        for h in range(8):
                            ins = e.matmul(PSY[:, h, :], lhsT=sbb[:, h, :], rhs=qb[:, h, :], start=True, stop=True)
                        return ins
                    P.op("pe", ymm, reads=["sbb", "qb"], writes=[("ps", 3), ("ps", 4)])
                    P.op("dve", lambda e, cs=cs: e.tensor_tensor(out=yp[:, :, cs], in0=yp[:, :, cs], in1=PSY, op=ALU.add),
                         reads=[("ps", 3), ("ps", 4), ("yp", c)], writes=[("yp", c)])

                    def kvm(e, c=c):
                        for h in range(8):
                            ins = e.matmul(PSS[:, h, :], lhsT=kb[:, h, :], rhs=vt[:, c, h * 128:(h + 1) * 128], start=True, stop=True)
                        return ins
                    P.op("pe", kvm, reads=["kb", "vt"], writes=[("ps", 1), ("ps", 2)])

                    def upd(e):
                        for h in range(8):
                            ins = e.scalar_tensor_tensor(out=sb[:, h, :], in0=sb[:, h, :], scalar=GCH[h], in1=PSS[:, h, :],
                                                         op0=ALU.mult, op1=ALU.add)
                        return ins
                    P.op("dve", upd, reads=["sb", ("ps", 1), ("ps", 2)], writes=["sb"])
                    if n == NCH // 2:
                        P.op("dve", lambda e: e.tensor_scalar(out=fl(sb), in0=fl(sb), scalar1=linkm[:, 0:1], scalar2=None,
                                                              op0=ALU.mult), reads=["sb", "linkm"], writes=["sb"])
                    P.op("act", lambda e: e.copy(out=fl(sbb), in_=fl(sb)), reads=["sb"], writes=["sbb"])
                ypr = [("yp", c) for c in range(4)]
                P.op("act", lambda e: e.activation(out=ysq, in_=yp, func=AF.Square), reads=ypr, writes=["ysq"])
                for h in range(8):
                    bank = 5 + ncnt[0] % 3
                    ri = ncnt[0] % 2
                    ncnt[0] += 1
                    P.op("pe", lambda e, h=h, bank=bank: e.matmul(PS(bank), lhsT=onesB, rhs=ysq[:, h, :], start=True, stop=True),
                         reads=["ysq", "onesB"], writes=[("ps", bank)])
                    P.op("act", lambda e, bank=bank, ri=ri: e.activation(out=rsn[ri], in_=PS(bank), func=AF.Sqrt,
                                                                         bias=epsT[:, 0:1], scale=1.0 / 128.0),
                         reads=[("ps", bank), "epsT"], writes=[("rsn", ri)])
                    P.op("dve", lambda e, ri=ri: e.reciprocal(out=rstd[ri], in_=rsn[ri]), reads=[("rsn", ri)], writes=[("rstd", ri)])
                    P.op("pool", lambda e, h=h, ri=ri: e.tensor_tensor(out=tmpn[ri], in0=yp[:, h, :], in1=rstd[ri], op=ALU.mult),
                         reads=ypr + [("rstd", ri)], writes=[("tmpn", ri)])
                    P.op("dve", lambda e, h=h, ri=ri: e.scalar_tensor_tensor(
                        out=mixb[:, h, :], in0=tmpn[ri], scalar=par("retn", l, h), in1=sg[:, h, :], op0=ALU.mult, op1=ALU.mult),
                        reads=[("tmpn", ri), "sg", "params"], writes=[("mixb", h)])
                def scan(e):
                    for blk in range(8):
                        ins = e.tensor_tensor_scan(out=rev(bb[:, blk, :]), data0=rev(ab[:, blk, :]), data1=rev(bb[:, blk, :]),
                                                   initial=carry[:, blk:blk + 1], op0=ALU.mult, op1=ALU.add)
                    return ins
                P.op("dve", scan, reads=["ab", "bb", "carry"], writes=["bb"])
                P.op("dve", lambda e: e.tensor_copy(out=carry, in_=bb[:, :, 0]), reads=["bb"], writes=["carry"])
                if t == HT:
                    P.op("dve", lambda e: e.tensor_scalar(out=carry, in0=carry, scalar1=linkm[:, 0:1], scalar2=None,
                                                          op0=ALU.mult), reads=["carry", "linkm"], writes=["carry"])
                P.op("pool", lambda e: e.tensor_tensor(out=hf, in0=hf, in1=bb, op=ALU.add), reads=["hf", "bb"], writes=["hf"])
                P.op("act", lambda e: e.activation(out=ysq, in_=hf, func=AF.Square), reads=["hf"], writes=["ysq"])
                bank = 5 + ncnt[0] % 3
                ri = ncnt[0] % 2
                ncnt[0] += 1

                def lsum(e):
                    for blk in range(8):
                        ins = e.matmul(PS(bank), lhsT=onesB, rhs=ysq[:, blk, :], start=(blk == 0), stop=(blk == 7))
                    return ins
                P.op("pe", lsum, reads=["ysq", "onesB"], writes=[("ps", bank)])
                P.op("act", lambda e: e.activation(out=rsn[ri], in_=PS(bank), func=AF.Sqrt, bias=epsT[:, 0:1], scale=1.0 / 1024.0),
                     reads=[("ps", bank), "epsT"], writes=[("rsn", ri)])
                P.op("dve", lambda e: e.reciprocal(out=rstd[ri], in_=rsn[ri]), reads=[("rsn", ri)], writes=[("rstd", ri)])
                for blk in range(8):
                    ti = blk % 2
                    P.op("pool", lambda e, blk=blk, ti=ti: e.tensor_tensor(out=tmpn[ti], in0=hf[:, blk, :], in1=rstd[ri], op=ALU.mult),
                         reads=["hf", ("rstd", ri)], writes=[("tmpn", ti)])
                    P.op("dve", lambda e, blk=blk, ti=ti: e.scalar_tensor_tensor(
                        out=mixb[:, 8 + blk, :], in0=tmpn[ti], scalar=par("lrun", l, blk), in1=gy[:, blk, :],
                        op0=ALU.mult, op1=ALU.mult),
                        reads=[("tmpn", ti), "gy", "params"], writes=[("mixb", 8 + blk)])
                P.op("sp", lambda e: e.dma_start(out=mix_s[t], in_=mixb), reads=[("mixb", j) for j in range(16)], chan="stm")

            for t in range(NT - 1, -1, -1):
                do_tile(t)
            P.barrier()

        def phase_c(l, last):
            A.reset()
            B16 = A.bf([16, 512])
            xC = A.f32([16, 512])
            ob = A.f32([16, 512])
            act = A.bf([44, 512])
            NR = 4
            ring = [A.bf([5632]) for _ in range(NR)]
            sgt = [A.f32([512]) for _ in range(2)]
            sqs = [A.bf([512]) for _ in range(2)]
            tmpx = [A.f32([512]) for _ in range(2)]
            rsn = A.f32([512])
            rstd = A.f32([512])
            NB = 6
            st = {"ld": 0, "use": 0, "ps": 0, "sg": 0, "sq": 0, "tx": 0}
            loads = []
            for t in range(NT):
                loads += [("out", s) for s in range(8)]
                for j in range(22):
                    loads += [("gate", j), ("up", j)]
                loads += [("down", s) for s in range(16)]
            srcs = {"out": wb_out, "gate": wb_gate, "up": wb_up, "down": wb_down}

            def issue_load():
                i = st["ld"]
                if i >= len(loads):
                    return
                st["ld"] += 1
                kind, s = loads[i]
                slot = i % NR
                src = srcs[kind][l, s].rearrange("p a b -> p (a b)")
                n = 44 * 128 if kind == "down" else 16 * 256
                P.op("sp", lambda e: e.dma_start(out=ring[slot][:, 0:n], in_=src),
                     reads=[("wb", kind, l)], writes=[("ring", slot)], chan=("ring", slot))

            def next_slab():
                i = st["use"]
                st["use"] += 1
                return i % NR

            for _ in range(NR):
                issue_load()

            def norm_from(bank_reads):
                P.op("act", lambda e: e.activation(out=rsn, in_=PS(6), func=AF.Sqrt, bias=epsT[:, 0:1], scale=1.0 / D),
                     reads=[("ps", 6), "epsT"], writes=["rsn"])
                P.op("dve", lambda e: e.reciprocal(out=rstd, in_=rsn), reads=["rsn"], writes=["rstd"])

            def proj_epilogue(bank, mt):
                P.op("act", lambda e: e.copy(out=ob[:, mt, :], in_=PS(bank)), reads=[("ps", bank)], writes=[("ob", mt)])
                qi = st["sq"] % 2
                st["sq"] += 1
                P.op("act", lambda e: e.activation(out=sqs[qi], in_=PS(bank), func=AF.Square),
                     reads=[("ps", bank)], writes=[("sqs", qi)])
                P.op("pe", lambda e: e.matmul(PS(6), lhsT=onesB, rhs=sqs[qi], start=(mt == 0), stop=(mt == 15)),
                     reads=[("sqs", qi), "onesB"], writes=[("ps", 6)])

            def residual(gname):
                for kc in range(16):
                    ti = st["tx"] % 2
                    st["tx"] += 1
                    P.op("dve", lambda e, kc=kc, ti=ti: e.scalar_tensor_tensor(
                        out=tmpx[ti], in0=ob[:, kc, :], scalar=par(gname, l, kc), in1=rstd, op0=ALU.mult, op1=ALU.mult),
                        reads=[("ob", kc), "rstd", "params"], writes=[("tmpx", ti)])
                    P.op("pool", lambda e, kc=kc, ti=ti: e.tensor_tensor(out=xC[:, kc, :], in0=xC[:, kc, :], in1=tmpx[ti], op=ALU.add),
                         reads=[("xC", kc), ("tmpx", ti)], writes=[("xC", kc)])

            def do_tile(t):
                b16r = [("B16", kc) for kc in range(16)]
                xcr = [("xC", kc) for kc in range(16)]
                P.op("sp", lambda e: e.dma_start(out=B16, in_=mix_s[t]), writes=b16r, chan="ldm")
                P.op("sp", lambda e: e.dma_start(out=xC, in_=xs[t]), writes=xcr, chan="ldx")
                for s in range(8):
                    slot = next_slab()
                    wv = ring[slot][:, 0:4096].rearrange("p (a b) -> p a b", b=256)
                    for m in range(2):
                        mt = s * 2 + m
                        bank = st["ps"] % NB
                        st["ps"] += 1

                        def mm(e, m=m, bank=bank, wv=wv):
                            for kc in range(16):
                                ins = e.matmul(PS(bank), lhsT=wv[:, kc, m * 128:(m + 1) * 128], rhs=B16[:, kc, :],
                                               start=(kc == 0), stop=(kc == 15))
                            return ins
                        P.op("pe", mm, reads=b16r + [("ring", slot)], writes=[("ps", bank)])
                        proj_epilogue(bank, mt)
                    issue_load()
                norm_from(None)
                residual("npo")
                P.op("act", lambda e: e.activation(out=B16, in_=xC, func=AF.Square), reads=xcr, writes=b16r)

                def ssq(e):
                    for kc in range(16):
                        ins = e.matmul(PS(6), lhsT=onesB, rhs=B16[:, kc, :], start=(kc == 0), stop=(kc == 15))
                    return ins
                P.op("pe", ssq, reads=b16r + ["onesB"], writes=[("ps", 6)])
                norm_from(None)
                for kc in range(16):
                    P.op("dve", lambda e, kc=kc: e.scalar_tensor_tensor(
                        out=B16[:, kc, :], in0=xC[:, kc, :], scalar=par("npf", l, kc), in1=rstd, op0=ALU.mult, op1=ALU.mult),
                        reads=[("xC", kc), "rstd", "params"], writes=[("B16", kc)])
                for j in range(22):
                    sg_ = next_slab()
                    su_ = next_slab()
                    wg = ring[sg_][:, 0:4096].rearrange("p (a b) -> p a b", b=256)
                    wu = ring[su_][:, 0:4096].rearrange("p (a b) -> p a b", b=256)
                    for m in range(2):
                        ft = j * 2 + m
                        bg = st["ps"] % NB
                        st["ps"] += 1
                        bu = st["ps"] % NB
                        st["ps"] += 1

                        def mmg(e, m=m, bg=bg, wg=wg):
                            for kc in range(16):
                                ins = e.matmul(PS(bg), lhsT=wg[:, kc, m * 128:(m + 1) * 128], rhs=B16[:, kc, :],
                                               start=(kc == 0), stop=(kc == 15))
                            return ins

                        def mmu(e, m=m, bu=bu, wu=wu):
                            for kc in range(16):
                                ins = e.matmul(PS(bu), lhsT=wu[:, kc, m * 128:(m + 1) * 128], rhs=B16[:, kc, :],
                                               start=(kc == 0), stop=(kc == 15))
                            return ins
                        P.op("pe", mmg, reads=b16r + [("ring", sg_)], writes=[("ps", bg)])
                        P.op("pe", mmu, reads=b16r + [("ring", su_)], writes=[("ps", bu)])
                        si = st["sg"] % 2
                        st["sg"] += 1
                        P.op("act", lambda e, bg=bg, si=si: e.activation(out=sgt[si], in_=PS(bg), func=AF.Silu),
                             reads=[("ps", bg)], writes=[("sgt", si)])
                        P.op("dve", lambda e, bu=bu, si=si, ft=ft: e.tensor_tensor(out=act[:, ft, :], in0=sgt[si], in1=PS(bu), op=ALU.mult),
                             reads=[("ps", bu), ("sgt", si)], writes=[("act", ft)])
                    issue_load()
                    issue_load()
                actr = [("act", ft) for ft in range(44)]
                for s in range(16):
                    slot = next_slab()
                    wd = ring[slot][:, 0:5632].rearrange("p (a b) -> p a b", b=128)
                    bank = st["ps"] % NB
                    st["ps"] += 1

                    def mmd(e, bank=bank, wd=wd):
                        for kf_ in range(44):
                            ins = e.matmul(PS(bank), lhsT=wd[:, kf_, :], rhs=act[:, kf_, :], start=(kf_ == 0), stop=(kf_ == 43))
                        return ins
                    P.op("pe", mmd, reads=actr + [("ring", slot)], writes=[("ps", bank)])
                    proj_epilogue(bank, s)
                    issue_load()
                norm_from(None)
                residual("npff")
                P.op("sp", lambda e: e.dma_start(out=xs[t], in_=xC), reads=xcr, chan="stx")

            for t in range(NT):
                do_tile(t)
            P.barrier()

        def phase_tout():
            A.reset()
            xt = [A.f32([16, 512]) for _ in range(2)]
            yo = [A.f32([4, 2048]) for _ in range(2)]
            cnt = [0]

            def do_tile(t):
                sl = t % 2
                P.op("sp", lambda e: e.dma_start(out=xt[sl], in_=xs[t]), writes=[("xt", sl)], chan=("ldxt", sl))
                for b in range(4):
                    for kq in range(4):
                        bank = cnt[0] % 6
                        cnt[0] += 1

                        def tr(e, b=b, kq=kq, bank=bank):
                            for k4 in range(4):
                                kc = kq * 4 + k4
                                ins = e.transpose(out=PS(bank)[:, k4 * 128:(k4 + 1) * 128],
                                                  in_=xt[sl][:, kc, b * 128:(b + 1) * 128], identity=identF)
                            return ins
                        P.op("pe", tr, reads=[("xt", sl), "identF"], writes=[("ps", bank)])
                        if (b * 4 + kq) % 2 == 0:
                            P.op("act", lambda e, b=b, kq=kq, bank=bank: e.copy(out=yo[sl][:, b, kq * 512:(kq + 1) * 512], in_=PS(bank)),
                                 reads=[("ps", bank)], writes=[("yo", sl, b, kq)])
                        else:
                            P.op("dve", lambda e, b=b, kq=kq, bank=bank: e.tensor_copy(out=yo[sl][:, b, kq * 512:(kq + 1) * 512], in_=PS(bank)),
                                 reads=[("ps", bank)], writes=[("yo", sl, b, kq)])
                dst = y_out[t * 512:(t + 1) * 512, :].rearrange("(b p) d -> p b d", p=128)
                P.op("sp", lambda e: e.dma_start(out=dst, in_=yo[sl]),
                     reads=[("yo", sl, b, kq) for b in range(4) for kq in range(4)], chan=("sty", sl))

            for t in range(NT):
                do_tile(t)
            P.barrier()

        phase_tin()
        stages = []
        for l in range(L):
            stages += [("a", l), ("b1", l), ("b2", l), ("c", l)]
        done = False
        if stop_after == "tin":
            done = True
        for kind, l in stages:
            if done:
                break
            if kind == "a":
                phase_a(l)
            elif kind == "b1":
                phase_b1(l)
            elif kind == "b2":
                phase_b2(l)
            else:
                if l + 1 < L:
                    convert_layer(l + 1)
                phase_c(l, l == L - 1)
            if stop_after == (kind, l):
                done = True
        if not done:
            phase_tout()
        P.barrier()

        block = es.enter_context(nc.Block())
        P.emit(nc, block, sems_ctx)
    return nc, outs, P


def _prep_inputs(inp, L, S):
    T = 2 * S
    ropes, cst, _ = _host_consts(S)
    params = _pack_params(inp, L)
    lruw = np.ascontiguousarray(np.stack([np.asarray(inp[k], np.float32)[:L] for k in
                                          ("lru_wa_fwd", "lru_wx_fwd", "lru_wa_bwd", "lru_wx_bwd")], 1))
    ident = np.eye(128, dtype=np.float32)
    shared = {
        "w_in": np.ascontiguousarray(np.asarray(inp["w_in"], np.float32)[:L]),
        "w_out": np.ascontiguousarray(np.asarray(inp["w_out"], np.float32)[:L]),
        "w_gate": np.ascontiguousarray(np.asarray(inp["w_gate"], np.float32)[:L]),
        "w_up": np.ascontiguousarray(np.asarray(inp["w_up"], np.float32)[:L]),
        "w_down": np.ascontiguousarray(np.asarray(inp["w_down"], np.float32)[:L]),
        "lruw": lruw, "params": params, "cst": cst, "ident": ident,
    }
    xp = np.asarray(inp["x_prompt"], np.float32)
    xsm = np.asarray(inp["x_sample"], np.float32)
    maps = []
    for c in range(NCORES):
        m = dict(shared)
        if c < 4:
            m["x"] = np.ascontiguousarray(xsm[c])
            m["rope"] = ropes[0]
            m["link"] = np.ones((128, 1), np.float32)
        else:
            i = c - 4
            m["x"] = np.ascontiguousarray(xp[2 * i:2 * i + 2].reshape(T, D))
            m["rope"] = ropes[1]
            m["link"] = np.zeros((128, 1), np.float32)
        maps.append(m)
    return maps


def run(inp, L, dbg=(), stop_after=None, trace=False, cores=None, verbose=False):
    import time
    S = int(np.asarray(inp["x_prompt"]).shape[1])
    assert np.asarray(inp["x_sample"]).shape[1] == 2 * S
    t0 = time.time()
    nc, outs, P = build_program(L, S, dbg=dbg, stop_after=stop_after)
    t1 = time.time()
    maps = _prep_inputs(inp, L, S)
    if cores is not None:
        maps = [maps[c] for c in cores]
    t2 = time.time()
    res = run_bass_kernel_spmd(nc, maps, core_ids=list(range(len(maps))), trace=trace)
    t3 = time.time()
    if verbose:
        print("build %.1fs prep %.1fs run %.1fs ops=%d sems=%d" % (t1 - t0, t2 - t1, t3 - t2, len(P.ops), P.n_sems), flush=True)
    extra = {n: [res.results[c][n] for c in range(len(maps))] for n in outs}
    if cores is not None:
        return [res.results[c]["y"] for c in range(len(maps))], extra, res
    ys = [res.results[c]["y"] for c in range(NCORES)]
    y_sample = np.stack(ys[:4], 0)
    y_prompt = np.concatenate([ys[4 + i].reshape(2, S, D) for i in range(4)], 0)
    return (y_prompt, y_sample), extra, res


def kernel(**inputs):
    (yp, ys), _, _ = run(inputs, 4)
    return yp, ys
```

```python
import numpy as np
import concourse.bass as bass
import concourse.mybir as mybir
from concourse.bass_utils import run_bass_kernel_spmd

F32 = mybir.dt.float32
BF16 = mybir.dt.bfloat16
ALU = mybir.AluOpType
AF = mybir.ActivationFunctionType

D = 2048
KD = 16
INW = 6144
DFF = 5632
KF = 44
H = 8
TT = 512
EPS = 1e-6
NCORES = 8

ENGS = ("sp", "act", "pe", "dve", "pool")


class Prog:
    def __init__(self):
        self.ops = []
        self.lw = {}
        self.rd = {}
        self.last_eng = {}
        self.last_chan = {}
        self.bar_excl = set()

    def op(self, eng, fn, reads=(), writes=(), chan=None):
        i = len(self.ops)
        deps = {}
        for r in reads:
            w = self.lw.get(r)
            if w is not None:
                deps[w] = "RAW"
        for r in writes:
            w = self.lw.get(r)
            if w is not None and w not in deps:
                deps[w] = "WAW"
            rr = self.rd.get(r)
            if rr:
                for q in rr[0].values():
                    if q not in deps:
                        deps[q] = "WAR"
                for q in rr[1]:
                    if q not in deps:
                        deps[q] = "WAR"
        for r in reads:
            rr = self.rd.get(r)
            if rr is None:
                rr = self.rd[r] = [{}, []]
            if chan is not None:
                rr[1].append(i)
            else:
                rr[0][eng] = i
        for r in writes:
            self.lw[r] = i
            self.rd[r] = [{}, []]
        self.ops.append([eng, fn, deps, chan])
        self.last_eng[eng] = i
        if chan is not None:
            self.last_chan[chan] = i
        return i

    def barrier(self):
        lasts = set(self.last_eng.values())
        for c, i in self.last_chan.items():
            if c not in self.bar_excl:
                lasts.add(i)
        for eng in ENGS:
            i = len(self.ops)
            deps = {w: "BAR" for w in lasts}
            self.ops.append([eng, None, deps, None])
            self.last_eng[eng] = i

    def emit(self, nc, block, sems_ctx):
        ops = self.ops
        waits = [None] * len(ops)
        need_signal = set()
        for i, (eng, fn, deps, chan) in enumerate(ops):
            wl = []
            for w, kind in deps.items():
                weng, _, _, wchan = ops[w]
                if ops[w][1] is None:
                    continue
                if wchan is not None or chan is not None or weng != eng:
                    need = True
                else:
                    need = (eng != "pe") and kind in ("RAW", "BAR")
                if need:
                    wl.append(w)
                    if wchan is None:
                        need_signal.add(w)
            waits[i] = wl
        tick = {}
        cnt = {}
        for i, (eng, fn, deps, chan) in enumerate(ops):
            if fn is None:
                continue
            if chan is not None:
                key = ("ch", chan)
                cnt[key] = cnt.get(key, 0) + 16
                tick[i] = (key, cnt[key])
            elif i in need_signal:
                key = ("eng", eng)
                cnt[key] = cnt.get(key, 0) + 1
                tick[i] = (key, cnt[key])
        semobj = {}
        for key in cnt:
            semobj[key] = sems_ctx(("s_%s_%s" % (key[0], str(key[1]))).replace(" ", "").replace("'", "").replace("(", "").replace(")", "").replace(",", "_"))
        self.n_sems = len(semobj)
        per_eng = {e: [] for e in ENGS}
        for i, o in enumerate(ops):
            per_eng[o[0]].append(i)

        def run(eng_name, e):
            seen = {}
            for i in per_eng[eng_name]:
                _, fn, _, chan = ops[i]
                for w in waits[i]:
                    key, val = tick[w]
                    if seen.get(key, 0) < val:
                        e.wait_ge(semobj[key], val)
                        seen[key] = val
                if fn is not None:
                    ins = fn(e)
                    if i in tick:
                        ins.then_inc(semobj[tick[i][0]], 16 if chan is not None else 1)

        @block.sync
        def _(e):
            run("sp", e)

        @block.scalar
        def _(e):
            run("act", e)

        @block.tensor
        def _(e):
            run("pe", e)

        @block.vector
        def _(e):
            run("dve", e)

        @block.gpsimd
        def _(e):
            run("pool", e)


def _merge(a, b):
    out = []
    ia = ib = 0
    na, nb = len(a), len(b)
    while ia < na or ib < nb:
        if ib >= nb or (ia < na and ia * nb <= ib * na):
            out.append(a[ia])
            ia += 1
        else:
            out.append(b[ib])
            ib += 1
    return out


class Arena:
    def __init__(self, tensor, nelem):
        self.t = tensor
        self.n = nelem
        self.base = 0
        self.cur = 0

    def mark(self):
        self.base = self.cur

    def reset(self):
        self.cur = self.base

    def _take(self, nbytes):
        nb = (nbytes + 63) // 64 * 64
        off = self.cur
        self.cur += nb // 2
        assert self.cur <= self.n, "arena overflow: %d > %d (bytes/partition)" % (self.cur * 2, self.n * 2)
        return off

    def bf(self, shape):
        n = int(np.prod(shape))
        off = self._take(n * 2)
        v = self.t[:, off:off + n]
        return self._shape(v, shape)

    def f32(self, shape):
        n = int(np.prod(shape))
        off = self._take(n * 4)
        v = self.t[:, off:off + 2 * n].bitcast(F32)
        return self._shape(v, shape)

    @staticmethod
    def _shape(v, shape):
        if len(shape) == 1:
            return v
        if len(shape) == 2:
            return v.rearrange("p (a b) -> p a b", b=shape[1])
        if len(shape) == 3:
            return v.rearrange("p (a b c) -> p a b c", b=shape[1], c=shape[2])
        raise ValueError(shape)


def _pcol(v, nchunk):
    v = np.asarray(v, np.float32)
    lead = v.shape[:-1]
    v = v.reshape(lead + (nchunk, 128))
    return np.moveaxis(v, -1, 0)


class ParamIdx:
    def __init__(self, L):
        self.L = L
        o = 0
        self.off = {}
        for name, n in (("npm", 16), ("npo", 16), ("npf", 16), ("npff", 16), ("retn", 8), ("lrun", 8),
                        ("cw0", 8), ("cw1", 8), ("cw2", 8), ("cw3", 8), ("cb", 8),
                        ("baf", 8), ("bxf", 8), ("bab", 8), ("bxb", 8), ("lamf", 8), ("lamb", 8)):
            self.off[name] = o
            self.w = n
            o += n * L
        self.n = o
        self.width = {k: (16 if k in ("npm", "npo", "npf", "npff") else 8) for k in self.off}

    def idx(self, name, l, c):
        return self.off[name] + l * self.width[name] + c


def _pack_params(inp, L):
    pi = ParamIdx(L)
    out = np.zeros((128, pi.n), np.float32)

    def put(name, arr, nchunk):
        a = _pcol(arr, nchunk)
        o = pi.off[name]
        out[:, o:o + L * nchunk] = a.reshape(128, L * nchunk)

    put("npm", inp["norm_pre_mix"][:L], 16)
    put("npo", inp["norm_post_mix"][:L], 16)
    put("npf", inp["norm_pre_ffn"][:L], 16)
    put("npff", inp["norm_post_ffn"][:L], 16)
    put("retn", np.asarray(inp["ret_norm"])[:L].reshape(L, 1024), 8)
    put("lrun", inp["lru_norm"][:L], 8)
    cw = np.asarray(inp["conv_w"])[:L]
    for t in range(4):
        put("cw%d" % t, cw[:, t], 8)
    put("cb", inp["conv_b"][:L], 8)
    put("baf", inp["lru_ba_fwd"][:L], 8)
    put("bxf", inp["lru_bx_fwd"][:L], 8)
    put("bab", inp["lru_ba_bwd"][:L], 8)
    put("bxb", inp["lru_bx_bwd"][:L], 8)
    put("lamf", inp["lru_lam_fwd"][:L], 8)
    put("lamb", inp["lru_lam_bwd"][:L], 8)
    return out


def _host_consts(S):
    T = 2 * S
    f32 = np.float32
    inv = (f32(10000.0) ** (-(np.arange(0, 128, 2, dtype=f32)) / f32(128))).astype(f32)
    ropes = []
    for linked in (True, False):
        pos = np.arange(T, dtype=f32) if linked else np.concatenate([np.arange(S, dtype=f32)] * 2)
        ang = (pos[:, None] * inv[None, :]).astype(f32)
        cos = np.cos(ang).astype(f32).T
        sin = np.sin(ang).astype(f32).T
        cosT = np.concatenate([cos, cos], 0)
        sinS = np.concatenate([sin, -sin], 0)
        ropes.append(np.stack([cosT, sinS], 0).astype(f32))
    log_g = np.log1p(-(f32(2.0) ** (-5.0 - np.arange(8, dtype=f32)))).astype(f32)
    pos = np.arange(128, dtype=f32)
    dmat = np.exp(log_g[:, None, None] * np.abs(pos[:, None] - pos[None, :])[None]).astype(f32)
    kf = np.exp(log_g[None, :] * (127.0 - pos)[:, None]).astype(f32)
    kb = np.exp(log_g[None, :] * pos[:, None]).astype(f32)
    qf = np.exp(log_g[None, :] * (pos + 1.0)[:, None]).astype(f32)
    qb = np.exp(log_g[None, :] * (128.0 - pos)[:, None]).astype(f32)
    cst = np.zeros((5, 128, 8, 128), f32)
    cst[0] = dmat.transpose(1, 0, 2)
    cst[1] = np.broadcast_to(kf[:, :, None], (128, 8, 128))
    cst[2] = np.broadcast_to(kb[:, :, None], (128, 8, 128))
    cst[3] = np.broadcast_to(qf.T[None, :, :], (128, 8, 128))
    cst[4] = np.broadcast_to(qb.T[None, :, :], (128, 8, 128))
    gch = [float(np.exp(log_g[h] * f32(128.0)).astype(f32)) for h in range(8)]
    return ropes, cst.reshape(5, 128, 1024), gch


def build_program(L, S, dbg=(), stop_after=None):
    T = 2 * S
    NT = T // TT
    HT = NT // 2
    NCH = T // 128
    pi = ParamIdx(L)
    _, _, GCH = _host_consts(128)

    nc = bass.Bass("TRN2", target_bir_lowering=False)
    outs = []

    def din(name, shape, dt=F32):
        return nc.dram_tensor(name, list(shape), dt, kind="ExternalInput").ap()

    def dscr(name, shape, dt):
        if name in dbg:
            outs.append(name)
            return nc.dram_tensor(name, list(shape), dt, kind="ExternalOutput").ap()
        return nc.dram_tensor(name, list(shape), dt, kind="Internal").ap()

    x_in = din("x", [T, D])
    y_out = nc.dram_tensor("y", [T, D], F32, kind="ExternalOutput").ap()
    w_in = din("w_in", [L, D, INW])
    w_out = din("w_out", [L, D, D])
    w_gate = din("w_gate", [L, D, DFF])
    w_up = din("w_up", [L, D, DFF])
    w_down = din("w_down", [L, DFF, D])
    lruw = din("lruw", [L, 4, 8, 128, 128])
    params_d = din("params", [128, pi.n])
    cst_d = din("cst", [5, 128, 1024])
    rope_d = din("rope", [2, 128, T])
    ident_d = din("ident", [128, 128])
    link_d = din("link", [128, 1])

    wb_in = dscr("wb_in", [L, 12, 128, 16, 512], BF16)
    wb_out = dscr("wb_out", [L, 8, 128, 16, 256], BF16)
    wb_gate = dscr("wb_gate", [L, 22, 128, 16, 256], BF16)
    wb_up = dscr("wb_up", [L, 22, 128, 16, 256], BF16)
    wb_down = dscr("wb_down", [L, 16, 128, 44, 128], BF16)
    xs = dscr("xs", [NT, 128, 16, 512], F32)
    q_s = dscr("q_s", [NT, 128, 8, 512], BF16)
    k_s = dscr("k_s", [NT, 128, 8, 512], BF16)
    g_s = dscr("g_s", [NT, 128, 8, 512], BF16)
    gy_s = dscr("gy_s", [NT, 128, 8, 512], BF16)
    v_s = dscr("v_s", [NCH, 128, 1024], BF16)
    ux_s = dscr("ux_s", [128, 8, T], F32)
    yp_s = dscr("yp_s", [NT, 128, 8, 512], F32)
    hf_s = dscr("hf_s", [NT, 128, 8, 512], F32)
    ab_s = dscr("ab_s", [NT, 128, 8, 512], F32)
    bb_s = dscr("bb_s", [NT, 128, 8, 512], F32)
    mix_s = dscr("mix_s", [NT, 128, 16, 512], BF16)

    P = Prog()
    ARENA_BYTES = 204 * 1024

    import contextlib
    es = contextlib.ExitStack()
    with es:
        arena_t = es.enter_context(nc.sbuf_tensor("arena", [128, ARENA_BYTES // 2], BF16))
        psum_t = es.enter_context(nc.psum_tensor("psum", [128, 8, 512], F32))
        A = Arena(arena_t, ARENA_BYTES // 2)

        def sems_ctx(name):
            return es.enter_context(nc.semaphore(name))

        def PS(b, n=1):
            return psum_t[:, b:b + n, :].rearrange("p a b -> p (a b)")

        identF = A.f32([128])
        identB = A.bf([128])
        onesB = A.bf([128])
        params = A.f32([pi.n])
        linkm = A.f32([1])
        epsT = A.f32([1])
        oneT = A.f32([1])
        c1t = A.f32([2 * L * 8])
        c2t = A.f32([2 * L * 8])
        A.mark()

        def par(name, l, c):
            j = pi.idx(name, l, c)
            return params[:, j:j + 1]

        P.op("sp", lambda e: e.dma_start(out=identF, in_=ident_d), writes=["identF"], chan="c0a")
        P.op("sp", lambda e: e.dma_start(out=params, in_=params_d), writes=["params"], chan="c0b")
        P.op("sp", lambda e: e.dma_start(out=linkm, in_=link_d), writes=["linkm"], chan="c0c")
        P.op("act", lambda e: e.copy(out=identB, in_=identF), reads=["identF"], writes=["identB"])
        P.op("dve", lambda e: e.memset(onesB, 1.0), writes=["onesB"])
        P.op("dve", lambda e: e.memset(epsT, EPS), writes=["epsT"])
        P.op("dve", lambda e: e.memset(oneT, 1.0), writes=["oneT"])
        lo = pi.off["lamf"]
        nl = 2 * L * 8
        P.op("act", lambda e: e.activation(out=c1t, in_=params[:, lo:lo + nl], func=AF.Exp, scale=-1.0),
             reads=["params"], writes=["c1t"])
        P.op("act", lambda e: e.activation(out=c1t, in_=c1t, func=AF.Ln, bias=oneT[:, 0:1], scale=1.0),
             reads=["c1t", "oneT"], writes=["c1t"])
        P.op("dve", lambda e: e.tensor_scalar(out=c1t, in0=c1t, scalar1=-8.0, scalar2=None, op0=ALU.mult),
             reads=["c1t"], writes=["c1t"])
        P.op("dve", lambda e: e.tensor_scalar(out=c2t, in0=c1t, scalar1=2.0, scalar2=None, op0=ALU.mult),
             reads=["c1t"], writes=["c2t"])

        def convert_layer(l):
            def cv(kind, dst, src, last):
                ch = ("wcv", l, kind)
                P.bar_excl.add(ch)
                P.op("pool", lambda e: e.dma_start(out=dst, in_=src),
                     writes=[("wb", kind, l) if last else ("wbp", kind, l, id(dst))], chan=ch)
            for s in range(12):
                cv("in", wb_in[l, s], w_in[l, :, s * 512:(s + 1) * 512].rearrange("(kc p) c -> p kc c", p=128), s == 11)
            for s in range(8):
                cv("out", wb_out[l, s], w_out[l, :, s * 256:(s + 1) * 256].rearrange("(kc p) c -> p kc c", p=128), s == 7)
            for s in range(22):
                cv("gate", wb_gate[l, s], w_gate[l, :, s * 256:(s + 1) * 256].rearrange("(kc p) c -> p kc c", p=128), s == 21)
                cv("up", wb_up[l, s], w_up[l, :, s * 256:(s + 1) * 256].rearrange("(kc p) c -> p kc c", p=128), s == 21)
            for s in range(16):
                cv("down", wb_down[l, s], w_down[l, :, s * 128:(s + 1) * 128].rearrange("(kc p) c -> p kc c", p=128), s == 15)

        convert_layer(0)

        def phase_tin():
            A.reset()
            xin = [A.f32([4, 2048]) for _ in range(2)]
            xt = [A.f32([16, 512]) for _ in range(2)]
            for t in range(NT):
                sl = t % 2
                src = x_in[t * 512:(t + 1) * 512, :].rearrange("(b p) d -> p b d", p=128)
                P.op("sp", lambda e, sl=sl, src=src: e.dma_start(out=xin[sl], in_=src),
                     writes=[("xin", sl)], chan=("xin", sl))
                for kc in range(16):
                    bank = kc % 4
                    def tr(e, sl=sl, kc=kc, bank=bank):
                        for b in range(4):
                            ins = e.transpose(out=PS(bank)[:, b * 128:(b + 1) * 128],
                                              in_=xin[sl][:, b, kc * 128:(kc + 1) * 128], identity=identF)
                        return ins
                    P.op("pe", tr, reads=[("xin", sl), "identF"], writes=[("ps", bank)])
                    eng = "act" if kc % 2 == 0 else "dve"
                    if eng == "act":
                        fn = lambda e, sl=sl, kc=kc, bank=bank: e.copy(out=xt[sl][:, kc, :], in_=PS(bank))
                    else:
                        fn = lambda e, sl=sl, kc=kc, bank=bank: e.tensor_copy(out=xt[sl][:, kc, :], in_=PS(bank))
                    P.op(eng, fn, reads=[("ps", bank)], writes=[("xt", sl, kc)])
                P.op("sp", lambda e, sl=sl, t=t: e.dma_start(out=xs[t], in_=xt[sl]),
                     reads=[("xt", sl, kc) for kc in range(16)], chan=("xst", sl))
            P.barrier()

        def phase_a(l):
            A.reset()
            xA = [A.f32([16, 512]) for _ in range(2)]
            sqb = A.bf([16, 512])
            hT = [A.bf([16, 512]) for _ in range(2)]
            rs = A.f32([512])
            rstd = A.f32([512])
            ropeC = [A.f32([512]) for _ in range(2)]
            ropeS = [A.f32([512]) for _ in range(2)]
            NWR = 3
            wr = [A.bf([16, 512]) for _ in range(NWR)]
            NSTG = 6
            stgb = [A.bf([512]) for _ in range(NSTG)]
            stgf = [A.f32([512]) for _ in range(4)]
            tmp1 = [A.f32([512]) for _ in range(3)]
            tmp2 = [A.f32([512]) for _ in range(3)]
            cnt = {"wr": 0, "ps": 0, "sb": 0, "sf": 0, "tm": 0}
            NPSB = 6
            SC = float(128.0 ** -0.5)

            def load_x(t):
                sl = t % 2
                P.op("sp", lambda e: e.dma_start(out=xA[sl], in_=xs[t]), writes=[("xA", sl)], chan=("xA", sl))
                P.op("sp", lambda e: e.dma_start(out=ropeC[sl], in_=rope_d[0, :, t * 512:(t + 1) * 512]),
                     writes=[("ropeC", sl)], chan=("rope", sl))
                P.op("sp", lambda e: e.dma_start(out=ropeS[sl], in_=rope_d[1, :, t * 512:(t + 1) * 512]),
                     writes=[("ropeS", sl)], chan=("ropeS", sl))

            def load_w(s):
                slot = cnt["wr"] % NWR
                cnt["wr"] += 1
                P.op("sp", lambda e: e.dma_start(out=wr[slot], in_=wb_in[l, s]),
                     reads=[("wb", "in", l)], writes=[("wr", slot)], chan=("wr", slot))
                return slot

            load_x(0)

            def norm_tile(t):
                sl = t % 2
                P.op("act", lambda e: e.activation(out=sqb, in_=xA[sl], func=AF.Square),
                     reads=[("xA", sl)], writes=["sqb"])

                def ssq(e):
                    for kc in range(16):
                        ins = e.matmul(PS(7), lhsT=onesB, rhs=sqb[:, kc, :], start=(kc == 0), stop=(kc == 15))
                    return ins
                P.op("pe", ssq, reads=["sqb", "onesB"], writes=[("ps", 7)])
                P.op("act", lambda e: e.activation(out=rs, in_=PS(7), func=AF.Ln, bias=epsT[:, 0:1], scale=1.0 / D),
                     reads=[("ps", 7), "epsT"], writes=["rs"])
                P.op("act", lambda e: e.activation(out=rstd, in_=rs, func=AF.Exp, scale=-0.5), reads=["rs"], writes=["rstd"])
                for kc in range(16):
                    P.op("dve", lambda e, kc=kc: e.scalar_tensor_tensor(
                        out=hT[sl][:, kc, :], in0=xA[sl][:, kc, :], scalar=par("npm", l, kc), in1=rstd,
                        op0=ALU.mult, op1=ALU.mult),
                        reads=[("xA", sl), "rstd", "params"], writes=[("hT", sl, kc)])

            def do_tile(t):
                sl = t % 2
                if t + 1 < NT:
                    load_x(t + 1)
                wslots = {}
                wslots[0] = load_w(0)
                wslots[1] = load_w(1)
                if t == 0:
                    norm_tile(0)
                hreads = [("hT", sl, kc) for kc in range(16)]
                for s in range(12):
                    if s + 2 < 12:
                        wslots[s + 2] = load_w(s + 2)
                    if s == 7 and t + 1 < NT:
                        norm_tile(t + 1)
                    ws = wslots[s]
                    kind = s // 2
                    if kind == 2:
                        for b in range(4):
                            bank = cnt["ps"] % NPSB
                            cnt["ps"] += 1

                            def mm(e, b=b, bank=bank, ws=ws):
                                for kc in range(16):
                                    ins = e.matmul(PS(bank), lhsT=hT[sl][:, kc, b * 128:(b + 1) * 128],
                                                   rhs=wr[ws][:, kc, :], start=(kc == 0), stop=(kc == 15))
                                return ins
                            P.op("pe", mm, reads=hreads + [("wr", ws)], writes=[("ps", bank)])
                            sb = cnt["sb"] % NSTG
                            cnt["sb"] += 1
                            P.op("act", lambda e, bank=bank, sb=sb: e.copy(out=stgb[sb], in_=PS(bank)),
                                 reads=[("ps", bank)], writes=[("stgb", sb)])
                            ch = t * 4 + b
                            c0 = (s - 4) * 512
                            P.op("sp", lambda e, sb=sb, ch=ch, c0=c0: e.dma_start(out=v_s[ch, :, c0:c0 + 512], in_=stgb[sb]),
                                 reads=[("stgb", sb)], chan=("stgb", sb))
                        continue
                    for m in range(4):
                        mt = (s % 2) * 4 + m
                        bank = cnt["ps"] % NPSB
                        cnt["ps"] += 1

                        def mm(e, m=m, bank=bank, ws=ws):
                            for kc in range(16):
                                ins = e.matmul(PS(bank), lhsT=wr[ws][:, kc, m * 128:(m + 1) * 128],
                                               rhs=hT[sl][:, kc, :], start=(kc == 0), stop=(kc == 15))
                            return ins
                        P.op("pe", mm, reads=hreads + [("wr", ws)], writes=[("ps", bank)])
                        if kind in (0, 1):
                            sc = 1.0 if kind == 0 else SC
                            ti = cnt["tm"] % 3
                            cnt["tm"] += 1
                            P.op("act", lambda e, bank=bank, ti=ti, sc=sc: e.activation(
                                out=tmp1[ti], in_=PS(bank), func=AF.Copy, scale=sc),
                                reads=[("ps", bank)], writes=[("tmp1", ti)])
                            P.op("dve", lambda e, ti=ti: e.tensor_tensor(
                                out=tmp2[ti][0:64, :], in0=tmp1[ti][64:128, :], in1=ropeS[sl][64:128, :], op=ALU.mult),
                                reads=[("tmp1", ti), ("ropeS", sl)], writes=[("tmp2", ti, 0)])
                            P.op("dve", lambda e, ti=ti: e.tensor_tensor(
                                out=tmp2[ti][64:128, :], in0=tmp1[ti][0:64, :], in1=ropeS[sl][0:64, :], op=ALU.mult),
                                reads=[("tmp1", ti), ("ropeS", sl)], writes=[("tmp2", ti, 1)])
                            P.op("pool", lambda e, ti=ti: e.tensor_tensor(
                                out=tmp1[ti], in0=tmp1[ti], in1=ropeC[sl], op=ALU.mult),
                                reads=[("tmp1", ti), ("ropeC", sl), ("tmp2", ti, 0), ("tmp2", ti, 1)], writes=[("tmp1", ti)])
                            sb = cnt["sb"] % NSTG
                            cnt["sb"] += 1
                            P.op("dve", lambda e, ti=ti, sb=sb: e.tensor_tensor(
                                out=stgb[sb], in0=tmp1[ti], in1=tmp2[ti], op=ALU.add),
                                reads=[("tmp1", ti), ("tmp2", ti, 0), ("tmp2", ti, 1)], writes=[("stgb", sb)])
                            dst = (q_s if kind == 0 else k_s)[t, :, mt, :]
                            P.op("sp", lambda e, sb=sb, dst=dst: e.dma_start(out=dst, in_=stgb[sb]),
                                 reads=[("stgb", sb)], chan=("stgb", sb))
                        elif kind in (3, 5):
                            fnc = AF.Silu if kind == 3 else AF.Gelu_apprx_tanh
                            sb = cnt["sb"] % NSTG
                            cnt["sb"] += 1
                            P.op("act", lambda e, bank=bank, sb=sb, fnc=fnc: e.activation(
                                out=stgb[sb], in_=PS(bank), func=fnc),
                                reads=[("ps", bank)], writes=[("stgb", sb)])
                            dst = (g_s if kind == 3 else gy_s)[t, :, mt, :]
                            P.op("sp", lambda e, sb=sb, dst=dst: e.dma_start(out=dst, in_=stgb[sb]),
                                 reads=[("stgb", sb)], chan=("stgb", sb))
                        else:
                            sf = cnt["sf"] % 4
                            cnt["sf"] += 1
                            P.op("dve", lambda e, bank=bank, sf=sf: e.tensor_copy(out=stgf[sf], in_=PS(bank)),
                                 reads=[("ps", bank)], writes=[("stgf", sf)])
                            dst = ux_s[:, mt, t * 512:(t + 1) * 512]
                            P.op("sp", lambda e, sf=sf, dst=dst: e.dma_start(out=dst, in_=stgf[sf]),
                                 reads=[("stgf", sf)], chan=("stgf", sf))
            for t in range(NT):
                do_tile(t)
            P.barrier()


        def phase_b1(l):
            A.reset()
            Dtab = A.f32([8, 128])
            kftab = A.f32([8, 128])
            qftab = A.f32([8, 128])
            gw = A.bf([4, 8, 128])
            qT = A.bf([8, 512])
            kT = A.bf([8, 512])
            vt = A.bf([4, 1024])
            yp = A.f32([8, 512])
            kf = A.bf([8, 128])
            Pm = A.bf([8, 128])
            qf = A.bf([8, 128])
            sf = A.f32([8, 128])
            sfb = A.bf([8, 128])
            uxw = A.f32([8, 515])
            uc = A.f32([8, 512])
            ucb = A.bf([8, 512])
            rr = A.f32([8, 512])
            ig = A.f32([8, 512])
            aa = A.f32([8, 512])
            hf = A.f32([8, 512])
            carry = A.f32([8])
            PST = PS(0).bitcast(BF16)
            PSS = PS(1, 2).rearrange("p (h i) -> p h i", i=128)
            PSY = PS(3, 2).rearrange("p (h i) -> p h i", i=128)
            fl = lambda ap: ap.rearrange("p h i -> p (h i)")

            P.op("sp", lambda e: e.dma_start(out=fl(Dtab), in_=cst_d[0]), writes=["Dtab"], chan="tb0")
            P.op("sp", lambda e: e.dma_start(out=fl(kftab), in_=cst_d[1]), writes=["kftab"], chan="tb1")
            P.op("sp", lambda e: e.dma_start(out=fl(qftab), in_=cst_d[3]), writes=["qftab"], chan="tb2")
            for k4 in range(4):
                P.op("pool", lambda e, k4=k4: e.dma_start(out=gw[:, k4], in_=lruw[l, k4].rearrange("g i j -> i g j")),
                     writes=[("gw", k4)], chan="gw")
            P.op("dve", lambda e: e.memset(fl(sf), 0.0), writes=["sf"])
            P.op("dve", lambda e: e.memset(fl(sfb), 0.0), writes=["sfb"])
            P.op("dve", lambda e: e.memset(carry, 0.0), writes=["carry"])
            gwr = [("gw", k4) for k4 in range(4)]

            def do_tile(t):
                Rl, Ul = [], []
                R = lambda *a, **k: Rl.append((a, k))
                U = lambda *a, **k: Ul.append((a, k))
                R("sp", lambda e: e.dma_start(out=qT, in_=q_s[t]), writes=["qT"], chan="ldq")
                R("sp", lambda e: e.dma_start(out=kT, in_=k_s[t]), writes=["kT"], chan="ldk")
                R("sp", lambda e: e.dma_start(out=vt, in_=v_s[t * 4:(t + 1) * 4].rearrange("c p e -> p c e")),
                     writes=["vt"], chan="ldv")
                lo = t * 512 - 2
                hi = t * 512 + 513
                wl, wh = 0, 515
                if lo < 0:
                    wl, lo = 2, 0
                    U("dve", lambda e: e.memset(uxw[:, :, 0:2], 0.0), writes=["uxw"])
                if hi > T:
                    wh, hi = 514, T
                    U("dve", lambda e: e.memset(uxw[:, :, 514:515], 0.0), writes=["uxw"])
                U("sp", lambda e: e.dma_start(out=uxw[:, :, wl:wh], in_=ux_s[:, :, lo:hi]), writes=["uxw"], chan="ldu")
                if t == HT:
                    U("dve", lambda e: e.tensor_scalar(out=uxw[:, :, 0:2], in0=uxw[:, :, 0:2], scalar1=linkm[:, 0:1],
                                                          scalar2=None, op0=ALU.mult), reads=["uxw", "linkm"], writes=["uxw"])
                if t == HT - 1:
                    U("dve", lambda e: e.tensor_scalar(out=uxw[:, :, 514:515], in0=uxw[:, :, 514:515], scalar1=linkm[:, 0:1],
                                                          scalar2=None, op0=ALU.mult), reads=["uxw", "linkm"], writes=["uxw"])
                for c in range(4):
                    n = t * 4 + c
                    cs = slice(c * 128, (c + 1) * 128)

                    def trk(e, cs=cs):
                        for h in range(8):
                            ins = e.transpose(out=PST[:, h * 128:(h + 1) * 128], in_=kT[:, h, cs], identity=identB)
                        return ins
                    R("pe", trk, reads=["kT", "identB"], writes=[("ps", 0)])
                    R("dve", lambda e: e.tensor_tensor(out=fl(kf), in0=PST, in1=fl(kftab), op=ALU.mult),
                         reads=[("ps", 0), "kftab"], writes=["kf"])

                    def sc(e, cs=cs):
                        for h in range(8):
                            ins = e.matmul(PSS[:, h, :], lhsT=kT[:, h, cs], rhs=qT[:, h, cs], start=True, stop=True)
                        return ins
                    R("pe", sc, reads=["kT", "qT"], writes=[("ps", 1), ("ps", 2)])
                    R("dve", lambda e: e.tensor_tensor(out=fl(Pm), in0=fl(PSS), in1=fl(Dtab), op=ALU.mult),
                         reads=[("ps", 1), ("ps", 2), "Dtab"], writes=["Pm"])
                    R("pool", lambda e, cs=cs: e.tensor_tensor(out=qf, in0=qT[:, :, cs], in1=qftab, op=ALU.mult),
                         reads=["qT", "qftab"], writes=["qf"])

                    def ymm(e, c=c):
                        for h in range(8):
                            e.matmul(PSY[:, h, :], lhsT=vt[:, c, h * 128:(h + 1) * 128], rhs=Pm[:, h, :], start=True, stop=False)
                            ins = e.matmul(PSY[:, h, :], lhsT=sfb[:, h, :], rhs=qf[:, h, :], start=False, stop=True)
                        return ins
                    R("pe", ymm, reads=["vt", "Pm", "sfb", "qf"], writes=[("ps", 3), ("ps", 4)])
                    R("act", lambda e, cs=cs: e.copy(out=yp[:, :, cs], in_=PSY),
                         reads=[("ps", 3), ("ps", 4)], writes=[("yp", c)])

                    def kvm(e, c=c):
                        for h in range(8):
                            ins = e.matmul(PSS[:, h, :], lhsT=kf[:, h, :], rhs=vt[:, c, h * 128:(h + 1) * 128], start=True, stop=True)
                        return ins
                    R("pe", kvm, reads=["kf", "vt"], writes=[("ps", 1), ("ps", 2)])

                    def upd(e):
                        for h in range(8):
                            ins = e.scalar_tensor_tensor(out=sf[:, h, :], in0=sf[:, h, :], scalar=GCH[h], in1=PSS[:, h, :],
                                                         op0=ALU.mult, op1=ALU.add)
                        return ins
                    R("dve", upd, reads=["sf", ("ps", 1), ("ps", 2)], writes=["sf"])
                    if n == NCH // 2 - 1:
                        R("dve", lambda e: e.tensor_scalar(out=fl(sf), in0=fl(sf), scalar1=linkm[:, 0:1], scalar2=None,
                                                              op0=ALU.mult), reads=["sf", "linkm"], writes=["sf"])
                    R("act", lambda e: e.copy(out=fl(sfb), in_=fl(sf)), reads=["sf"], writes=["sfb"])
                R("sp", lambda e: e.dma_start(out=yp_s[t], in_=yp), reads=[("yp", c) for c in range(4)], chan="sty")
                for blk in range(8):
                    U("act", lambda e, blk=blk: e.activation(out=uc[:, blk, :], in_=uxw[:, blk, 0:512], func=AF.Identity,
                                                                bias=par("cb", l, blk), scale=par("cw0", l, blk)),
                         reads=["uxw", "params"], writes=[("uc", blk)])
                    for tap in (1, 2, 3):
                        U("dve", lambda e, blk=blk, tap=tap: e.scalar_tensor_tensor(
                            out=uc[:, blk, :], in0=uxw[:, blk, tap:tap + 512], scalar=par("cw%d" % tap, l, blk),
                            in1=uc[:, blk, :], op0=ALU.mult, op1=ALU.add),
                            reads=["uxw", "params", ("uc", blk)], writes=[("uc", blk)])
                ucr = [("uc", blk) for blk in range(8)]
                U("act", lambda e: e.copy(out=ucb, in_=uc), reads=ucr, writes=["ucb"])
                gcnt = [0]
                for d in range(2):
                    for k2 in range(2):
                        kind = d * 2 + k2
                        dest = rr if k2 == 0 else ig
                        dname = "rr" if k2 == 0 else "ig"
                        bname = ("baf", "bxf", "bab", "bxb")[kind]
                        for blk in range(8):
                            bank = 5 + gcnt[0] % 3
                            gcnt[0] += 1
                            U("pe", lambda e, kind=kind, blk=blk, bank=bank: e.matmul(
                                PS(bank), lhsT=gw[:, kind, blk, :], rhs=ucb[:, blk, :], start=True, stop=True),
                                reads=["ucb"] + gwr, writes=[("ps", bank)])
                            U("act", lambda e, dest=dest, blk=blk, bank=bank, bname=bname: e.activation(
                                out=dest[:, blk, :], in_=PS(bank), func=AF.Sigmoid, bias=par(bname, l, blk), scale=1.0),
                                reads=[("ps", bank), "params"], writes=[(dname, blk)])
                    for blk in range(8):
                        j = d * L * 8 + l * 8 + blk
                        U("act", lambda e, blk=blk, j=j: e.activation(out=aa[:, blk, :], in_=rr[:, blk, :], func=AF.Exp,
                                                                         scale=c1t[:, j:j + 1]),
                             reads=[("rr", blk), "c1t"], writes=[("aa", blk)])
                        U("act", lambda e, blk=blk, j=j: e.activation(out=rr[:, blk, :], in_=rr[:, blk, :], func=AF.Exp,
                                                                         scale=c2t[:, j:j + 1]),
                             reads=[("rr", blk), "c2t"], writes=[("rr", blk)])
                    rrr = [("rr", blk) for blk in range(8)]
                    igr = [("ig", blk) for blk in range(8)]
                    aar = [("aa", blk) for blk in range(8)]
                    for blk in range(8):
                        U("pool" if blk % 4 == 3 else "dve",
                          lambda e, blk=blk: e.tensor_tensor(out=ig[:, blk, :], in0=ig[:, blk, :], in1=uc[:, blk, :], op=ALU.mult),
                          reads=[("ig", blk), ("uc", blk)], writes=[("ig", blk)])
                    for half in range(2):
                        hs = slice(half * 4, half * 4 + 4)
                        hr = [("rr", blk) for blk in range(half * 4, half * 4 + 4)]
                        U("act", lambda e, hs=hs: e.activation(out=rr[:, hs, :], in_=rr[:, hs, :], func=AF.Sqrt, bias=oneT[:, 0:1], scale=-1.0),
                          reads=hr + ["oneT"], writes=hr)
                    for blk in range(8):
                        U("pool" if blk % 4 == 3 else "dve",
                          lambda e, blk=blk: e.tensor_tensor(out=ig[:, blk, :], in0=ig[:, blk, :], in1=rr[:, blk, :], op=ALU.mult),
                          reads=[("ig", blk), ("rr", blk)], writes=[("ig", blk)])
                    if d == 0:
                        for blk in range(8):
                            U("dve", lambda e, blk=blk: e.tensor_tensor_scan(
                                out=hf[:, blk, :], data0=aa[:, blk, :], data1=ig[:, blk, :],
                                initial=carry[:, blk:blk + 1], op0=ALU.mult, op1=ALU.add),
                              reads=[("aa", blk), ("ig", blk), "carry"], writes=[("hf", blk)])
                        hfr = [("hf", blk) for blk in range(8)]
                        U("dve", lambda e: e.tensor_copy(out=carry, in_=hf[:, :, 511]), reads=hfr, writes=["carry"])
                        if t == HT - 1:
                            U("dve", lambda e: e.tensor_scalar(out=carry, in0=carry, scalar1=linkm[:, 0:1], scalar2=None,
                                                                  op0=ALU.mult), reads=["carry", "linkm"], writes=["carry"])
                        U("sp", lambda e: e.dma_start(out=hf_s[t], in_=hf), reads=hfr, chan="sth")
                    else:
                        U("sp", lambda e: e.dma_start(out=ab_s[t], in_=aa), reads=aar, chan="sta")
                        U("sp", lambda e: e.dma_start(out=bb_s[t], in_=ig), reads=igr, chan="stb")

                for a, k in _merge(Rl, Ul):
                    P.op(*a, **k)

            for t in range(NT):
                do_tile(t)
            P.barrier()

        def phase_b2(l):
            A.reset()
            kbtab = A.f32([8, 128])
            qbtab = A.f32([8, 128])
            qT = A.bf([8, 512])
            kT = A.bf([8, 512])
            vt = A.bf([4, 1024])
            yp = A.f32([8, 512])
            sg = A.bf([8, 512])
            gy = A.bf([8, 512])
            ab = A.f32([8, 512])
            bb = A.f32([8, 512])
            hf = A.f32([8, 512])
            kb = A.bf([8, 128])
            qb = A.bf([8, 128])
            sb = A.f32([8, 128])
            sbb = A.bf([8, 128])
            ysq = A.bf([8, 512])
            mixb = A.bf([16, 512])
            rsn = [A.f32([512]) for _ in range(2)]
            rstd = [A.f32([512]) for _ in range(2)]
            tmpn = [A.f32([512]) for _ in range(2)]
            carry = A.f32([8])
            hsq = A.bf([8, 512])
            rsnU = A.f32([512])
            rstdU = A.f32([512])
            tmpnU = [A.f32([512]) for _ in range(2)]
            PST = PS(0).bitcast(BF16)
            PSS = PS(1, 2).rearrange("p (h i) -> p h i", i=128)
            PSY = PS(3, 2).rearrange("p (h i) -> p h i", i=128)
            fl = lambda ap: ap.rearrange("p h i -> p (h i)")
            rev = lambda ap: bass.AP(ap.tensor, ap.offset + (ap.ap[-1][1] - 1) * ap.ap[-1][0],
                                     [list(x) for x in ap.ap[:-1]] + [[-ap.ap[-1][0], ap.ap[-1][1]]])

            P.op("sp", lambda e: e.dma_start(out=fl(kbtab), in_=cst_d[2]), writes=["kbtab"], chan="tb0")
            P.op("sp", lambda e: e.dma_start(out=fl(qbtab), in_=cst_d[4]), writes=["qbtab"], chan="tb1")
            P.op("dve", lambda e: e.memset(fl(sb), 0.0), writes=["sb"])
            P.op("dve", lambda e: e.memset(fl(sbb), 0.0), writes=["sbb"])
            P.op("dve", lambda e: e.memset(carry, 0.0), writes=["carry"])
            ncnt = [0]

            def do_tile(t):
                Rl, Ul = [], []
                R = lambda *a, **k: Rl.append((a, k))
                U = lambda *a, **k: Ul.append((a, k))
                R("sp", lambda e: e.dma_start(out=qT, in_=q_s[t]), writes=["qT"], chan="ldq")
                R("sp", lambda e: e.dma_start(out=kT, in_=k_s[t]), writes=["kT"], chan="ldk")
                R("sp", lambda e: e.dma_start(out=vt, in_=v_s[t * 4:(t + 1) * 4].rearrange("c p e -> p c e")),
                     writes=["vt"], chan="ldv")
                R("sp", lambda e: e.dma_start(out=yp, in_=yp_s[t]), writes=[("yp", c) for c in range(4)], chan="ldy")
                R("sp", lambda e: e.dma_start(out=sg, in_=g_s[t]), writes=["sg"], chan="ldg")
                U("sp", lambda e: e.dma_start(out=gy, in_=gy_s[t]), writes=["gy"], chan="ldgy")
                U("sp", lambda e: e.dma_start(out=ab, in_=ab_s[t]), writes=["ab"], chan="lda")
                U("sp", lambda e: e.dma_start(out=bb, in_=bb_s[t]), writes=["bb"], chan="ldb")
                U("sp", lambda e: e.dma_start(out=hf, in_=hf_s[t]), writes=["hf"], chan="ldh")
                for c in (3, 2, 1, 0):
                    n = t * 4 + c
                    cs = slice(c * 128, (c + 1) * 128)

                    def trk(e, cs=cs):
                        for h in range(8):
                            ins = e.transpose(out=PST[:, h * 128:(h + 1) * 128], in_=kT[:, h, cs], identity=identB)
                        return ins
                    R("pe", trk, reads=["kT", "identB"], writes=[("ps", 0)])
                    R("dve", lambda e: e.tensor_tensor(out=fl(kb), in0=PST, in1=fl(kbtab), op=ALU.mult),
                         reads=[("ps", 0), "kbtab"], writes=["kb"])
                    R("pool", lambda e, cs=cs: e.tensor_tensor(out=qb, in0=qT[:, :, cs], in1=qbtab, op=ALU.mult),
                         reads=["qT", "qbtab"], writes=["qb"])

                    def ymm(e):
                        for h in range(8):
                            ins = e.matmul(PSY[:, h, :], lhsT=sbb[:, h, :], rhs=qb[:, h, :], start=True, stop=True)
                        return ins
                    R("pe", ymm, reads=["sbb", "qb"], writes=[("ps", 3), ("ps", 4)])
                    R("dve", lambda e, cs=cs: e.tensor_tensor(out=yp[:, :, cs], in0=yp[:, :, cs], in1=PSY, op=ALU.add),
                         reads=[("ps", 3), ("ps", 4), ("yp", c)], writes=[("yp", c)])

                    def kvm(e, c=c):
                        for h in range(8):
                            ins = e.matmul(PSS[:, h, :], lhsT=kb[:, h, :], rhs=vt[:, c, h * 128:(h + 1) * 128], start=True, stop=True)
                        return ins
                    R("pe", kvm, reads=["kb", "vt"], writes=[("ps", 1), ("ps", 2)])

                    def upd(e):
                        for h in range(8):
                            ins = e.scalar_tensor_tensor(out=sb[:, h, :], in0=sb[:, h, :], scalar=GCH[h], in1=PSS[:, h, :],
                                                         op0=ALU.mult, op1=ALU.add)
                        return ins
                    R("dve", upd, reads=["sb", ("ps", 1), ("ps", 2)], writes=["sb"])
                    if n == NCH // 2:
                        R("dve", lambda e: e.tensor_scalar(out=fl(sb), in0=fl(sb), scalar1=linkm[:, 0:1], scalar2=None,
                                                              op0=ALU.mult), reads=["sb", "linkm"], writes=["sb"])
                    R("act", lambda e: e.copy(out=fl(sbb), in_=fl(sb)), reads=["sb"], writes=["sbb"])
                ypr = [("yp", c) for c in range(4)]
                R("act", lambda e: e.activation(out=ysq, in_=yp, func=AF.Square), reads=ypr, writes=["ysq"])
                for h in range(8):
                    bank = 5 + ncnt[0] % 2
                    ri = ncnt[0] % 2
                    ncnt[0] += 1
                    R("pe", lambda e, h=h, bank=bank: e.matmul(PS(bank), lhsT=onesB, rhs=ysq[:, h, :], start=True, stop=True),
                         reads=["ysq", "onesB"], writes=[("ps", bank)])
                    R("act", lambda e, bank=bank, ri=ri: e.activation(out=rsn[ri], in_=PS(bank), func=AF.Ln,
                                                                         bias=epsT[:, 0:1], scale=1.0 / 128.0),
                         reads=[("ps", bank), "epsT"], writes=[("rsn", ri)])
                    R("act", lambda e, ri=ri: e.activation(out=rstd[ri], in_=rsn[ri], func=AF.Exp, scale=-0.5),
                         reads=[("rsn", ri)], writes=[("rstd", ri)])
                    R("pool", lambda e, h=h, ri=ri: e.tensor_tensor(out=tmpn[ri], in0=yp[:, h, :], in1=rstd[ri], op=ALU.mult),
                         reads=ypr + [("rstd", ri)], writes=[("tmpn", ri)])
                    R("dve", lambda e, h=h, ri=ri: e.scalar_tensor_tensor(
                        out=mixb[:, h, :], in0=tmpn[ri], scalar=par("retn", l, h), in1=sg[:, h, :], op0=ALU.mult, op1=ALU.mult),
                        reads=[("tmpn", ri), "sg", "params"], writes=[("mixb", h)])
                def scan(e):
                    for blk in range(8):
                        ins = e.tensor_tensor_scan(out=rev(bb[:, blk, :]), data0=rev(ab[:, blk, :]), data1=rev(bb[:, blk, :]),
                                                   initial=carry[:, blk:blk + 1], op0=ALU.mult, op1=ALU.add)
                    return ins
                U("dve", scan, reads=["ab", "bb", "carry"], writes=["bb"])
                U("dve", lambda e: e.tensor_copy(out=carry, in_=bb[:, :, 0]), reads=["bb"], writes=["carry"])
                if t == HT:
                    U("dve", lambda e: e.tensor_scalar(out=carry, in0=carry, scalar1=linkm[:, 0:1], scalar2=None,
                                                          op0=ALU.mult), reads=["carry", "linkm"], writes=["carry"])
                U("pool", lambda e: e.tensor_tensor(out=hf, in0=hf, in1=bb, op=ALU.add), reads=["hf", "bb"], writes=["hf"])
                U("act", lambda e: e.activation(out=hsq, in_=hf, func=AF.Square), reads=["hf"], writes=["hsq"])
                bank = 7

                def lsum(e):
                    for blk in range(8):
                        ins = e.matmul(PS(bank), lhsT=onesB, rhs=hsq[:, blk, :], start=(blk == 0), stop=(blk == 7))
                    return ins
                U("pe", lsum, reads=["hsq", "onesB"], writes=[("ps", bank)])
                U("act", lambda e: e.activation(out=rsnU, in_=PS(bank), func=AF.Ln, bias=epsT[:, 0:1], scale=1.0 / 1024.0),
                     reads=[("ps", bank), "epsT"], writes=["rsnU"])
                U("act", lambda e: e.activation(out=rstdU, in_=rsnU, func=AF.Exp, scale=-0.5), reads=["rsnU"], writes=["rstdU"])
                for blk in range(8):
                    ti = blk % 2
                    U("pool", lambda e, blk=blk, ti=ti: e.tensor_tensor(out=tmpnU[ti], in0=hf[:, blk, :], in1=rstdU, op=ALU.mult),
                         reads=["hf", "rstdU"], writes=[("tmpnU", ti)])
                    U("dve", lambda e, blk=blk, ti=ti: e.scalar_tensor_tensor(
                        out=mixb[:, 8 + blk, :], in0=tmpnU[ti], scalar=par("lrun", l, blk), in1=gy[:, blk, :],
                        op0=ALU.mult, op1=ALU.mult),
                        reads=[("tmpnU", ti), "gy", "params"], writes=[("mixb", 8 + blk)])
                for a, k in _merge(Rl, Ul):
                    P.op(*a, **k)

                P.op("sp", lambda e: e.dma_start(out=mix_s[t], in_=mixb), reads=[("mixb", j) for j in range(16)], chan="stm")

            for t in range(NT - 1, -1, -1):
                do_tile(t)
            P.barrier()

        def phase_c(l, last):
            A.reset()
            B16 = A.bf([16, 512])
            xC = A.f32([16, 512])
            ob = A.f32([16, 512])
            act = A.bf([44, 512])
            NR = 4
            ring = [A.bf([5632]) for _ in range(NR)]
            sgt = [A.f32([512]) for _ in range(2)]
            sqs = [A.bf([512]) for _ in range(4)]
            tmpx = [A.f32([512]) for _ in range(3)]
            rsn = A.f32([512])
            rstd = A.f32([512])
            NB = 6
            st = {"ld": 0, "use": 0, "ps": 0, "sg": 0, "sq": 0, "tx": 0}
            loads = []
            for t in range(NT):
                loads += [("out", s) for s in range(8)]
                for j in range(22):
                    loads += [("gate", j), ("up", j)]
                loads += [("down", s) for s in range(16)]
            srcs = {"out": wb_out, "gate": wb_gate, "up": wb_up, "down": wb_down}

            def issue_load():
                i = st["ld"]
                if i >= len(loads):
                    return
                st["ld"] += 1
                kind, s = loads[i]
                slot = i % NR
                src = srcs[kind][l, s].rearrange("p a b -> p (a b)")
                n = 44 * 128 if kind == "down" else 16 * 256
                P.op("sp", lambda e: e.dma_start(out=ring[slot][:, 0:n], in_=src),
                     reads=[("wb", kind, l)], writes=[("ring", slot)], chan=("ring", slot))

            def next_slab():
                i = st["use"]
                st["use"] += 1
                return i % NR

            for _ in range(NR):
                issue_load()

            def norm_from(bank_reads):
                P.op("act", lambda e: e.activation(out=rsn, in_=PS(6), func=AF.Ln, bias=epsT[:, 0:1], scale=1.0 / D),
                     reads=[("ps", 6), "epsT"], writes=["rsn"])
                P.op("act", lambda e: e.activation(out=rstd, in_=rsn, func=AF.Exp, scale=-0.5), reads=["rsn"], writes=["rstd"])

            pend = []

            def flush_sq(keep):
                while len(pend) > keep:
                    qi, mt = pend.pop(0)
                    P.op("pe", lambda e, qi=qi, mt=mt: e.matmul(PS(6), lhsT=onesB, rhs=sqs[qi], start=(mt == 0), stop=(mt == 15)),
                         reads=[("sqs", qi), "onesB"], writes=[("ps", 6)])

            def proj_epilogue(bank, mt):
                P.op("act", lambda e: e.copy(out=ob[:, mt, :], in_=PS(bank)), reads=[("ps", bank)], writes=[("ob", mt)])
                qi = st["sq"] % 4
                st["sq"] += 1
                P.op("act", lambda e: e.activation(out=sqs[qi], in_=PS(bank), func=AF.Square),
                     reads=[("ps", bank)], writes=[("sqs", qi)])
                pend.append((qi, mt))

            def residual(gname, sqdst=None):
                for kc in range(16):
                    ti = st["tx"] % 3
                    st["tx"] += 1
                    if sqdst is not None and kc >= 2:
                        k2 = kc - 2
                        P.op("pe", lambda e, k2=k2: e.matmul(PS(6), lhsT=onesB, rhs=sqdst[:, k2, :], start=(k2 == 0), stop=False),
                             reads=[("B16", k2), "onesB"], writes=[("ps", 6)])
                    P.op("dve", lambda e, kc=kc, ti=ti: e.scalar_tensor_tensor(
                        out=tmpx[ti], in0=ob[:, kc, :], scalar=par(gname, l, kc), in1=rstd, op0=ALU.mult, op1=ALU.mult),
                        reads=[("ob", kc), "rstd", "params"], writes=[("tmpx", ti)])
                    P.op("pool" if kc % 3 == 0 else "dve",
                         lambda e, kc=kc, ti=ti: e.tensor_tensor(out=xC[:, kc, :], in0=xC[:, kc, :], in1=tmpx[ti], op=ALU.add),
                         reads=[("xC", kc), ("tmpx", ti)], writes=[("xC", kc)])
                    if sqdst is not None:
                        P.op("act", lambda e, kc=kc: e.activation(out=sqdst[:, kc, :], in_=xC[:, kc, :], func=AF.Square),
                             reads=[("xC", kc)], writes=[("B16", kc)])
                if sqdst is not None:
                    for k2 in (14, 15):
                        P.op("pe", lambda e, k2=k2: e.matmul(PS(6), lhsT=onesB, rhs=sqdst[:, k2, :], start=False, stop=(k2 == 15)),
                             reads=[("B16", k2), "onesB"], writes=[("ps", 6)])

            def do_tile(t):
                b16r = [("B16", kc) for kc in range(16)]
                xcr = [("xC", kc) for kc in range(16)]
                if t == 0:
                    P.op("sp", lambda e: e.dma_start(out=B16, in_=mix_s[t]), writes=b16r, chan="ldm")
                P.op("sp", lambda e: e.dma_start(out=xC, in_=xs[t]), writes=xcr, chan="ldx")
                for s in range(8):
                    slot = next_slab()
                    wv = ring[slot][:, 0:4096].rearrange("p (a b) -> p a b", b=256)
                    for m in range(2):
                        mt = s * 2 + m
                        bank = st["ps"] % NB
                        st["ps"] += 1

                        def mm(e, m=m, bank=bank, wv=wv):
                            for kc in range(16):
                                ins = e.matmul(PS(bank), lhsT=wv[:, kc, m * 128:(m + 1) * 128], rhs=B16[:, kc, :],
                                               start=(kc == 0), stop=(kc == 15))
                            return ins
                        P.op("pe", mm, reads=b16r + [("ring", slot)], writes=[("ps", bank)])
                        flush_sq(2)
                        proj_epilogue(bank, mt)
                    issue_load()
                flush_sq(0)
                norm_from(None)
                residual("npo", sqdst=B16)
                norm_from(None)
                for kc in range(16):
                    P.op("dve", lambda e, kc=kc: e.scalar_tensor_tensor(
                        out=B16[:, kc, :], in0=xC[:, kc, :], scalar=par("npf", l, kc), in1=rstd, op0=ALU.mult, op1=ALU.mult),
                        reads=[("xC", kc), "rstd", "params"], writes=[("B16", kc)])
                for j in range(22):
                    sg_ = next_slab()
                    su_ = next_slab()
                    wg = ring[sg_][:, 0:4096].rearrange("p (a b) -> p a b", b=256)
                    wu = ring[su_][:, 0:4096].rearrange("p (a b) -> p a b", b=256)
                    for m in range(2):
                        ft = j * 2 + m
                        bg = st["ps"] % NB
                        st["ps"] += 1
                        bu = st["ps"] % NB
                        st["ps"] += 1

                        def mmg(e, m=m, bg=bg, wg=wg):
                            for kc in range(16):
                                ins = e.matmul(PS(bg), lhsT=wg[:, kc, m * 128:(m + 1) * 128], rhs=B16[:, kc, :],
                                               start=(kc == 0), stop=(kc == 15))
                            return ins

                        def mmu(e, m=m, bu=bu, wu=wu):
                            for kc in range(16):
                                ins = e.matmul(PS(bu), lhsT=wu[:, kc, m * 128:(m + 1) * 128], rhs=B16[:, kc, :],
                                               start=(kc == 0), stop=(kc == 15))
                            return ins
                        P.op("pe", mmg, reads=b16r + [("ring", sg_)], writes=[("ps", bg)])
                        P.op("pe", mmu, reads=b16r + [("ring", su_)], writes=[("ps", bu)])
                        si = st["sg"] % 2
                        st["sg"] += 1
                        P.op("act", lambda e, bg=bg, si=si: e.activation(out=sgt[si], in_=PS(bg), func=AF.Silu),
                             reads=[("ps", bg)], writes=[("sgt", si)])
                        P.op("dve", lambda e, bu=bu, si=si, ft=ft: e.tensor_tensor(out=act[:, ft, :], in0=sgt[si], in1=PS(bu), op=ALU.mult),
                             reads=[("ps", bu), ("sgt", si)], writes=[("act", ft)])
                    issue_load()
                    issue_load()
                if t + 1 < NT:
                    P.op("sp", lambda e: e.dma_start(out=B16, in_=mix_s[t + 1]), writes=b16r, chan="ldm")
                actr = [("act", ft) for ft in range(44)]
                for s in range(16):
                    slot = next_slab()
                    wd = ring[slot][:, 0:5632].rearrange("p (a b) -> p a b", b=128)
                    bank = st["ps"] % NB
                    st["ps"] += 1

                    def mmd(e, bank=bank, wd=wd):
                        for kf_ in range(44):
                            ins = e.matmul(PS(bank), lhsT=wd[:, kf_, :], rhs=act[:, kf_, :], start=(kf_ == 0), stop=(kf_ == 43))
                        return ins
                    P.op("pe", mmd, reads=actr + [("ring", slot)], writes=[("ps", bank)])
                    flush_sq(1)
                    proj_epilogue(bank, s)
                    issue_load()
                flush_sq(0)
                norm_from(None)
                residual("npff")
                P.op("sp", lambda e: e.dma_start(out=xs[t], in_=xC), reads=xcr, chan="stx")

            for t in range(NT):
                do_tile(t)
            P.barrier()

        def phase_tout():
            A.reset()
            xt = [A.f32([16, 512]) for _ in range(2)]
            yo = [A.f32([4, 2048]) for _ in range(2)]
            cnt = [0]

            def do_tile(t):
                sl = t % 2
                P.op("sp", lambda e: e.dma_start(out=xt[sl], in_=xs[t]), writes=[("xt", sl)], chan=("ldxt", sl))
                for b in range(4):
                    for kq in range(4):
                        bank = cnt[0] % 6
                        cnt[0] += 1

                        def tr(e, b=b, kq=kq, bank=bank):
                            for k4 in range(4):
                                kc = kq * 4 + k4
                                ins = e.transpose(out=PS(bank)[:, k4 * 128:(k4 + 1) * 128],
                                                  in_=xt[sl][:, kc, b * 128:(b + 1) * 128], identity=identF)
                            return ins
                        P.op("pe", tr, reads=[("xt", sl), "identF"], writes=[("ps", bank)])
                        if (b * 4 + kq) % 2 == 0:
                            P.op("act", lambda e, b=b, kq=kq, bank=bank: e.copy(out=yo[sl][:, b, kq * 512:(kq + 1) * 512], in_=PS(bank)),
                                 reads=[("ps", bank)], writes=[("yo", sl, b, kq)])
                        else:
                            P.op("dve", lambda e, b=b, kq=kq, bank=bank: e.tensor_copy(out=yo[sl][:, b, kq * 512:(kq + 1) * 512], in_=PS(bank)),
                                 reads=[("ps", bank)], writes=[("yo", sl, b, kq)])
                dst = y_out[t * 512:(t + 1) * 512, :].rearrange("(b p) d -> p b d", p=128)
                P.op("sp", lambda e: e.dma_start(out=dst, in_=yo[sl]),
                     reads=[("yo", sl, b, kq) for b in range(4) for kq in range(4)], chan=("sty", sl))

            for t in range(NT):
                do_tile(t)
            P.barrier()

        phase_tin()
        stages = []
        for l in range(L):
            stages += [("a", l), ("b1", l), ("b2", l), ("c", l)]
        done = False
        if stop_after == "tin":
            done = True
        for kind, l in stages:
            if done:
                break
            if kind == "a":
                phase_a(l)
            elif kind == "b1":
                phase_b1(l)
            elif kind == "b2":
                phase_b2(l)
            else:
                if l + 1 < L:
                    convert_layer(l + 1)
                phase_c(l, l == L - 1)
            if stop_after == (kind, l):
                done = True
        if not done:
            phase_tout()
        P.barrier()

        block = es.enter_context(nc.Block())
        P.emit(nc, block, sems_ctx)
    return nc, outs, P


def _prep_inputs(inp, L, S):
    T = 2 * S
    ropes, cst, _ = _host_consts(S)
    params = _pack_params(inp, L)
    lruw = np.ascontiguousarray(np.stack([np.asarray(inp[k], np.float32)[:L] for k in
                                          ("lru_wa_fwd", "lru_wx_fwd", "lru_wa_bwd", "lru_wx_bwd")], 1))
    ident = np.eye(128, dtype=np.float32)
    shared = {
        "w_in": np.ascontiguousarray(np.asarray(inp["w_in"], np.float32)[:L]),
        "w_out": np.ascontiguousarray(np.asarray(inp["w_out"], np.float32)[:L]),
        "w_gate": np.ascontiguousarray(np.asarray(inp["w_gate"], np.float32)[:L]),
        "w_up": np.ascontiguousarray(np.asarray(inp["w_up"], np.float32)[:L]),
        "w_down": np.ascontiguousarray(np.asarray(inp["w_down"], np.float32)[:L]),
        "lruw": lruw, "params": params, "cst": cst, "ident": ident,
    }
    xp = np.asarray(inp["x_prompt"], np.float32)
    xsm = np.asarray(inp["x_sample"], np.float32)
    maps = []
    for c in range(NCORES):
        m = dict(shared)
        if c < 4:
            m["x"] = np.ascontiguousarray(xsm[c])
            m["rope"] = ropes[0]
            m["link"] = np.ones((128, 1), np.float32)
        else:
            i = c - 4
            m["x"] = np.ascontiguousarray(xp[2 * i:2 * i + 2].reshape(T, D))
            m["rope"] = ropes[1]
            m["link"] = np.zeros((128, 1), np.float32)
        maps.append(m)
    return maps


def run(inp, L, dbg=(), stop_after=None, trace=False, cores=None, verbose=False):
    import time
    S = int(np.asarray(inp["x_prompt"]).shape[1])
    assert np.asarray(inp["x_sample"]).shape[1] == 2 * S
    t0 = time.time()
    nc, outs, P = build_program(L, S, dbg=dbg, stop_after=stop_after)
    t1 = time.time()
    maps = _prep_inputs(inp, L, S)
    if cores is not None:
        maps = [maps[c] for c in cores]
    t2 = time.time()
    res = run_bass_kernel_spmd(nc, maps, core_ids=list(range(len(maps))), trace=trace)
    t3 = time.time()
    if verbose:
        print("build %.1fs prep %.1fs run %.1fs ops=%d sems=%d" % (t1 - t0, t2 - t1, t3 - t2, len(P.ops), P.n_sems), flush=True)
    extra = {n: [res.results[c][n] for c in range(len(maps))] for n in outs}
    if cores is not None:
        return [res.results[c]["y"] for c in range(len(maps))], extra, res
    ys = [res.results[c]["y"] for c in range(NCORES)]
    y_sample = np.stack(ys[:4], 0)
    y_prompt = np.concatenate([ys[4 + i].reshape(2, S, D) for i in range(4)], 0)
    return (y_prompt, y_sample), extra, res


def kernel(**inputs):
    (yp, ys), _, _ = run(inputs, 4)
    return yp, ys
```

```python
import numpy as np
import concourse.bass as bass
import concourse.mybir as mybir
from concourse.bass_utils import run_bass_kernel_spmd

F32 = mybir.dt.float32
BF16 = mybir.dt.bfloat16
ALU = mybir.AluOpType
AF = mybir.ActivationFunctionType

D = 2048
KD = 16
INW = 6144
DFF = 5632
KF = 44
H = 8
TT = 512
EPS = 1e-6
NCORES = 8

ENGS = ("sp", "act", "pe", "dve", "pool")


class Prog:
    def __init__(self):
        self.ops = []
        self.lw = {}
        self.rd = {}
        self.last_eng = {}
        self.last_chan = {}
        self.bar_excl = set()

    def op(self, eng, fn, reads=(), writes=(), chan=None):
        i = len(self.ops)
        deps = {}
        for r in reads:
            w = self.lw.get(r)
            if w is not None:
                deps[w] = "RAW"
        for r in writes:
            w = self.lw.get(r)
            if w is not None and w not in deps:
                deps[w] = "WAW"
            rr = self.rd.get(r)
            if rr:
                for q in rr[0].values():
                    if q not in deps:
                        deps[q] = "WAR"
                for q in rr[1]:
                    if q not in deps:
                        deps[q] = "WAR"
        for r in reads:
            rr = self.rd.get(r)
            if rr is None:
                rr = self.rd[r] = [{}, []]
            if chan is not None:
                rr[1].append(i)
            else:
                rr[0][eng] = i
        for r in writes:
            self.lw[r] = i
            self.rd[r] = [{}, []]
        self.ops.append([eng, fn, deps, chan])
        self.last_eng[eng] = i
        if chan is not None:
            self.last_chan[chan] = i
        return i

    def barrier(self):
        lasts = set(self.last_eng.values())
        for c, i in self.last_chan.items():
            if c not in self.bar_excl:
                lasts.add(i)
        for eng in ENGS:
            i = len(self.ops)
            deps = {w: "BAR" for w in lasts}
            self.ops.append([eng, None, deps, None])
            self.last_eng[eng] = i

    def emit(self, nc, block, sems_ctx):
        ops = self.ops
        waits = [None] * len(ops)
        need_signal = set()
        for i, (eng, fn, deps, chan) in enumerate(ops):
            wl = []
            for w, kind in deps.items():
                weng, _, _, wchan = ops[w]
                if ops[w][1] is None:
                    continue
                if wchan is not None or chan is not None or weng != eng:
                    need = True
                else:
                    need = (eng != "pe") and kind in ("RAW", "BAR")
                if need:
                    wl.append(w)
                    if wchan is None:
                        need_signal.add(w)
            waits[i] = wl
        tick = {}
        cnt = {}
        for i, (eng, fn, deps, chan) in enumerate(ops):
            if fn is None:
                continue
            if chan is not None:
                key = ("ch", chan)
                cnt[key] = cnt.get(key, 0) + 16
                tick[i] = (key, cnt[key])
            elif i in need_signal:
                key = ("eng", eng)
                cnt[key] = cnt.get(key, 0) + 1
                tick[i] = (key, cnt[key])
        semobj = {}
        for key in cnt:
            semobj[key] = sems_ctx(("s_%s_%s" % (key[0], str(key[1]))).replace(" ", "").replace("'", "").replace("(", "").replace(")", "").replace(",", "_"))
        self.n_sems = len(semobj)
        per_eng = {e: [] for e in ENGS}
        for i, o in enumerate(ops):
            per_eng[o[0]].append(i)

        def run(eng_name, e):
            seen = {}
            for i in per_eng[eng_name]:
                _, fn, _, chan = ops[i]
                for w in waits[i]:
                    key, val = tick[w]
                    if seen.get(key, 0) < val:
                        e.wait_ge(semobj[key], val)
                        seen[key] = val
                if fn is not None:
                    ins = fn(e)
                    if i in tick:
                        ins.then_inc(semobj[tick[i][0]], 16 if chan is not None else 1)

        @block.sync
        def _(e):
            run("sp", e)

        @block.scalar
        def _(e):
            run("act", e)

        @block.tensor
        def _(e):
            run("pe", e)

        @block.vector
        def _(e):
            run("dve", e)

        @block.gpsimd
        def _(e):
            run("pool", e)


def _merge(a, b):
    out = []
    ia = ib = 0
    na, nb = len(a), len(b)
    while ia < na or ib < nb:
        if ib >= nb or (ia < na and ia * nb <= ib * na):
            out.append(a[ia])
            ia += 1
        else:
            out.append(b[ib])
            ib += 1
    return out


class Arena:
    def __init__(self, tensor, nelem):
        self.t = tensor
        self.n = nelem
        self.base = 0
        self.cur = 0

    def mark(self):
        self.base = self.cur

    def reset(self):
        self.cur = self.base

    def _take(self, nbytes):
        nb = (nbytes + 63) // 64 * 64
        off = self.cur
        self.cur += nb // 2
        assert self.cur <= self.n, "arena overflow: %d > %d (bytes/partition)" % (self.cur * 2, self.n * 2)
        return off

    def bf(self, shape):
        n = int(np.prod(shape))
        off = self._take(n * 2)
        v = self.t[:, off:off + n]
        return self._shape(v, shape)

    def f32(self, shape):
        n = int(np.prod(shape))
        off = self._take(n * 4)
        v = self.t[:, off:off + 2 * n].bitcast(F32)
        return self._shape(v, shape)

    @staticmethod
    def _shape(v, shape):
        if len(shape) == 1:
            return v
        if len(shape) == 2:
            return v.rearrange("p (a b) -> p a b", b=shape[1])
        if len(shape) == 3:
            return v.rearrange("p (a b c) -> p a b c", b=shape[1], c=shape[2])
        raise ValueError(shape)


def _pcol(v, nchunk):
    v = np.asarray(v, np.float32)
    lead = v.shape[:-1]
    v = v.reshape(lead + (nchunk, 128))
    return np.moveaxis(v, -1, 0)


class ParamIdx:
    def __init__(self, L):
        self.L = L
        o = 0
        self.off = {}
        for name, n in (("npm", 16), ("npo", 16), ("npf", 16), ("npff", 16), ("retn", 8), ("lrun", 8),
                        ("cw0", 8), ("cw1", 8), ("cw2", 8), ("cw3", 8), ("cb", 8),
                        ("baf", 8), ("bxf", 8), ("bab", 8), ("bxb", 8), ("lamf", 8), ("lamb", 8)):
            self.off[name] = o
            self.w = n
            o += n * L
        self.n = o
        self.width = {k: (16 if k in ("npm", "npo", "npf", "npff") else 8) for k in self.off}

    def idx(self, name, l, c):
        return self.off[name] + l * self.width[name] + c


def _pack_params(inp, L):
    pi = ParamIdx(L)
    out = np.zeros((128, pi.n), np.float32)

    def put(name, arr, nchunk):
        a = _pcol(arr, nchunk)
        o = pi.off[name]
        out[:, o:o + L * nchunk] = a.reshape(128, L * nchunk)

    put("npm", inp["norm_pre_mix"][:L], 16)
    put("npo", inp["norm_post_mix"][:L], 16)
    put("npf", inp["norm_pre_ffn"][:L], 16)
    put("npff", inp["norm_post_ffn"][:L], 16)
    put("retn", np.asarray(inp["ret_norm"])[:L].reshape(L, 1024), 8)
    put("lrun", inp["lru_norm"][:L], 8)
    cw = np.asarray(inp["conv_w"])[:L]
    for t in range(4):
        put("cw%d" % t, cw[:, t], 8)
    put("cb", inp["conv_b"][:L], 8)
    put("baf", inp["lru_ba_fwd"][:L], 8)
    put("bxf", inp["lru_bx_fwd"][:L], 8)
    put("bab", inp["lru_ba_bwd"][:L], 8)
    put("bxb", inp["lru_bx_bwd"][:L], 8)
    put("lamf", inp["lru_lam_fwd"][:L], 8)
    put("lamb", inp["lru_lam_bwd"][:L], 8)
    return out


def _host_consts(S):
    T = 2 * S
    f32 = np.float32
    inv = (f32(10000.0) ** (-(np.arange(0, 128, 2, dtype=f32)) / f32(128))).astype(f32)
    ropes = []
    for linked in (True, False):
        pos = np.arange(T, dtype=f32) if linked else np.concatenate([np.arange(S, dtype=f32)] * 2)
        ang = (pos[:, None] * inv[None, :]).astype(f32)
        cos = np.cos(ang).astype(f32).T
        sin = np.sin(ang).astype(f32).T
        cosT = np.concatenate([cos, cos], 0)
        sinS = np.concatenate([sin, -sin], 0)
        ropes.append(np.stack([cosT, sinS], 0).astype(f32))
    log_g = np.log1p(-(f32(2.0) ** (-5.0 - np.arange(8, dtype=f32)))).astype(f32)
    pos = np.arange(128, dtype=f32)
    dmat = np.exp(log_g[:, None, None] * np.abs(pos[:, None] - pos[None, :])[None]).astype(f32)
    kf = np.exp(log_g[None, :] * (127.0 - pos)[:, None]).astype(f32)
    kb = np.exp(log_g[None, :] * pos[:, None]).astype(f32)
    qf = np.exp(log_g[None, :] * (pos + 1.0)[:, None]).astype(f32)
    qb = np.exp(log_g[None, :] * (128.0 - pos)[:, None]).astype(f32)
    cst = np.zeros((5, 128, 8, 128), f32)
    cst[0] = dmat.transpose(1, 0, 2)
    cst[1] = np.broadcast_to(kf[:, :, None], (128, 8, 128))
    cst[2] = np.broadcast_to(kb[:, :, None], (128, 8, 128))
    cst[3] = np.broadcast_to(qf.T[None, :, :], (128, 8, 128))
    cst[4] = np.broadcast_to(qb.T[None, :, :], (128, 8, 128))
    gch = [float(np.exp(log_g[h] * f32(128.0)).astype(f32)) for h in range(8)]
    return ropes, cst.reshape(5, 128, 1024), gch


def build_program(L, S, dbg=(), stop_after=None):
    T = 2 * S
    NT = T // TT
    HT = NT // 2
    NCH = T // 128
    pi = ParamIdx(L)
    _, _, GCH = _host_consts(128)

    nc = bass.Bass("TRN2", target_bir_lowering=False)
    outs = []

    def din(name, shape, dt=F32):
        return nc.dram_tensor(name, list(shape), dt, kind="ExternalInput").ap()

    def dscr(name, shape, dt):
        if name in dbg:
            outs.append(name)
            return nc.dram_tensor(name, list(shape), dt, kind="ExternalOutput").ap()
        return nc.dram_tensor(name, list(shape), dt, kind="Internal").ap()

    x_in = din("x", [T, D])
    y_out = nc.dram_tensor("y", [T, D], F32, kind="ExternalOutput").ap()
    w_in = din("w_in", [L, D, INW])
    w_out = din("w_out", [L, D, D])
    w_gate = din("w_gate", [L, D, DFF])
    w_up = din("w_up", [L, D, DFF])
    w_down = din("w_down", [L, DFF, D])
    lruw = din("lruw", [L, 4, 8, 128, 128])
    params_d = din("params", [128, pi.n])
    cst_d = din("cst", [5, 128, 1024])
    rope_d = din("rope", [2, 128, T])
    ident_d = din("ident", [128, 128])
    link_d = din("link", [128, 1])

    wb_in = dscr("wb_in", [L, 12, 128, 16, 512], BF16)
    wb_out = dscr("wb_out", [L, 8, 128, 16, 256], BF16)
    wb_gate = dscr("wb_gate", [L, 22, 128, 16, 256], BF16)
    wb_up = dscr("wb_up", [L, 22, 128, 16, 256], BF16)
    wb_down = dscr("wb_down", [L, 16, 128, 44, 128], BF16)
    xs = dscr("xs", [NT, 128, 16, 512], F32)
    q_s = dscr("q_s", [NT, 128, 8, 512], BF16)
    k_s = dscr("k_s", [NT, 128, 8, 512], BF16)
    g_s = dscr("g_s", [NT, 128, 8, 512], BF16)
    gy_s = dscr("gy_s", [NT, 128, 8, 512], BF16)
    v_s = dscr("v_s", [NCH, 128, 1024], BF16)
    ux_s = dscr("ux_s", [128, 8, T], F32)
    yp_s = dscr("yp_s", [NT, 128, 8, 512], F32)
    hf_s = dscr("hf_s", [NT, 128, 8, 512], F32)
    ab_s = dscr("ab_s", [NT, 128, 8, 512], F32)
    bb_s = dscr("bb_s", [NT, 128, 8, 512], F32)
    mix_s = dscr("mix_s", [NT, 128, 16, 512], BF16)

    P = Prog()
    ARENA_BYTES = 204 * 1024

    import contextlib
    es = contextlib.ExitStack()
    with es:
        arena_t = es.enter_context(nc.sbuf_tensor("arena", [128, ARENA_BYTES // 2], BF16))
        psum_t = es.enter_context(nc.psum_tensor("psum", [128, 8, 512], F32))
        A = Arena(arena_t, ARENA_BYTES // 2)

        def sems_ctx(name):
            return es.enter_context(nc.semaphore(name))

        def PS(b, n=1):
            return psum_t[:, b:b + n, :].rearrange("p a b -> p (a b)")

        identF = A.f32([128])
        identB = A.bf([128])
        onesB = A.bf([128])
        params = A.f32([pi.n])
        linkm = A.f32([1])
        epsT = A.f32([1])
        oneT = A.f32([1])
        c1t = A.f32([2 * L * 8])
        c2t = A.f32([2 * L * 8])
        A.mark()

        def par(name, l, c):
            j = pi.idx(name, l, c)
            return params[:, j:j + 1]

        P.op("sp", lambda e: e.dma_start(out=identF, in_=ident_d), writes=["identF"], chan="c0a")
        P.op("sp", lambda e: e.dma_start(out=params, in_=params_d), writes=["params"], chan="c0b")
        P.op("sp", lambda e: e.dma_start(out=linkm, in_=link_d), writes=["linkm"], chan="c0c")
        P.op("act", lambda e: e.copy(out=identB, in_=identF), reads=["identF"], writes=["identB"])
        P.op("dve", lambda e: e.memset(onesB, 1.0), writes=["onesB"])
        P.op("dve", lambda e: e.memset(epsT, EPS), writes=["epsT"])
        P.op("dve", lambda e: e.memset(oneT, 1.0), writes=["oneT"])
        lo = pi.off["lamf"]
        nl = 2 * L * 8
        P.op("act", lambda e: e.activation(out=c1t, in_=params[:, lo:lo + nl], func=AF.Exp, scale=-1.0),
             reads=["params"], writes=["c1t"])
        P.op("act", lambda e: e.activation(out=c1t, in_=c1t, func=AF.Ln, bias=oneT[:, 0:1], scale=1.0),
             reads=["c1t", "oneT"], writes=["c1t"])
        P.op("dve", lambda e: e.tensor_scalar(out=c1t, in0=c1t, scalar1=-8.0, scalar2=None, op0=ALU.mult),
             reads=["c1t"], writes=["c1t"])
        P.op("dve", lambda e: e.tensor_scalar(out=c2t, in0=c1t, scalar1=2.0, scalar2=None, op0=ALU.mult),
             reads=["c1t"], writes=["c2t"])

        def convert_layer(l):
            def cv(kind, dst, src, last):
                ch = ("wcv", l, kind)
                P.bar_excl.add(ch)
                P.op("pool", lambda e: e.dma_start(out=dst, in_=src),
                     writes=[("wb", kind, l) if last else ("wbp", kind, l, id(dst))], chan=ch)
            for s in range(12):
                cv("in", wb_in[l, s], w_in[l, :, s * 512:(s + 1) * 512].rearrange("(kc p) c -> p kc c", p=128), s == 11)
            for s in range(8):
                cv("out", wb_out[l, s], w_out[l, :, s * 256:(s + 1) * 256].rearrange("(kc p) c -> p kc c", p=128), s == 7)
            for s in range(22):
                cv("gate", wb_gate[l, s], w_gate[l, :, s * 256:(s + 1) * 256].rearrange("(kc p) c -> p kc c", p=128), s == 21)
                cv("up", wb_up[l, s], w_up[l, :, s * 256:(s + 1) * 256].rearrange("(kc p) c -> p kc c", p=128), s == 21)
            for s in range(16):
                cv("down", wb_down[l, s], w_down[l, :, s * 128:(s + 1) * 128].rearrange("(kc p) c -> p kc c", p=128), s == 15)

        convert_layer(0)

        def phase_tin():
            A.reset()
            xin = [A.f32([4, 2048]) for _ in range(2)]
            xt = [A.f32([16, 512]) for _ in range(2)]
            for t in range(NT):
                sl = t % 2
                src = x_in[t * 512:(t + 1) * 512, :].rearrange("(b p) d -> p b d", p=128)
                P.op("sp", lambda e, sl=sl, src=src: e.dma_start(out=xin[sl], in_=src),
                     writes=[("xin", sl)], chan=("xin", sl))
                for kc in range(16):
                    bank = kc % 4
                    def tr(e, sl=sl, kc=kc, bank=bank):
                        for b in range(4):
                            ins = e.transpose(out=PS(bank)[:, b * 128:(b + 1) * 128],
                                              in_=xin[sl][:, b, kc * 128:(kc + 1) * 128], identity=identF)
                        return ins
                    P.op("pe", tr, reads=[("xin", sl), "identF"], writes=[("ps", bank)])
                    eng = "act" if kc % 2 == 0 else "dve"
                    if eng == "act":
                        fn = lambda e, sl=sl, kc=kc, bank=bank: e.copy(out=xt[sl][:, kc, :], in_=PS(bank))
                    else:
                        fn = lambda e, sl=sl, kc=kc, bank=bank: e.tensor_copy(out=xt[sl][:, kc, :], in_=PS(bank))
                    P.op(eng, fn, reads=[("ps", bank)], writes=[("xt", sl, kc)])
                P.op("sp", lambda e, sl=sl, t=t: e.dma_start(out=xs[t], in_=xt[sl]),
                     reads=[("xt", sl, kc) for kc in range(16)], chan=("xst", sl))
            P.barrier()

        def phase_a(l):
            A.reset()
            xA = [A.f32([16, 512])]
            sqb = A.bf([16, 512])
            hT = [A.bf([16, 512]) for _ in range(2)]
            rs = A.f32([512])
            rstd = A.f32([512])
            ropeC = [A.f32([512]) for _ in range(2)]
            ropeS = [A.f32([512]) for _ in range(2)]
            NWR = 4
            wr = [A.bf([16, 512]) for _ in range(NWR)]
            NSTG = 6
            stgb = [A.bf([512]) for _ in range(NSTG)]
            stgf = [A.f32([512]) for _ in range(4)]
            tmp1 = [A.f32([512]) for _ in range(3)]
            tmp2 = [A.f32([512]) for _ in range(3)]
            cnt = {"wr": 0, "ps": 0, "sb": 0, "sf": 0, "tm": 0}
            NPSB = 6
            SC = float(128.0 ** -0.5)

            def load_x(t):
                sl = t % 2
                P.op("sp", lambda e: e.dma_start(out=xA[0], in_=xs[t]), writes=[("xA", 0)], chan=("xA", 0))
                P.op("sp", lambda e: e.dma_start(out=ropeC[sl], in_=rope_d[0, :, t * 512:(t + 1) * 512]),
                     writes=[("ropeC", sl)], chan=("rope", sl))
                P.op("sp", lambda e: e.dma_start(out=ropeS[sl], in_=rope_d[1, :, t * 512:(t + 1) * 512]),
                     writes=[("ropeS", sl)], chan=("ropeS", sl))

            def load_w(s):
                slot = cnt["wr"] % NWR
                cnt["wr"] += 1
                P.op("sp", lambda e: e.dma_start(out=wr[slot], in_=wb_in[l, s]),
                     reads=[("wb", "in", l)], writes=[("wr", slot)], chan=("wr", slot))
                return slot

            load_x(0)

            def norm_tile(t):
                sl = t % 2
                P.op("act", lambda e: e.activation(out=sqb, in_=xA[0], func=AF.Square),
                     reads=[("xA", 0)], writes=["sqb"])

                def ssq(e):
                    for kc in range(16):
                        ins = e.matmul(PS(7), lhsT=onesB, rhs=sqb[:, kc, :], start=(kc == 0), stop=(kc == 15))
                    return ins
                P.op("pe", ssq, reads=["sqb", "onesB"], writes=[("ps", 7)])
                P.op("act", lambda e: e.activation(out=rs, in_=PS(7), func=AF.Ln, bias=epsT[:, 0:1], scale=1.0 / D),
                     reads=[("ps", 7), "epsT"], writes=["rs"])
                P.op("act", lambda e: e.activation(out=rstd, in_=rs, func=AF.Exp, scale=-0.5), reads=["rs"], writes=["rstd"])
                for kc in range(16):
                    P.op("dve", lambda e, kc=kc: e.scalar_tensor_tensor(
                        out=hT[sl][:, kc, :], in0=xA[0][:, kc, :], scalar=par("npm", l, kc), in1=rstd,
                        op0=ALU.mult, op1=ALU.mult),
                        reads=[("xA", 0), "rstd", "params"], writes=[("hT", sl, kc)])

            wslots = {}
            LA = 3

            def prefetch(t, s):
                j = s + LA
                if j < 12:
                    wslots[(t, j)] = load_w(j)
                elif t + 1 < NT:
                    wslots[(t + 1, j - 12)] = load_w(j - 12)

            def do_tile(t):
                sl = t % 2
                if t == 0:
                    for j in range(LA):
                        wslots[(0, j)] = load_w(j)
                    norm_tile(0)
                if t + 1 < NT:
                    load_x(t + 1)
                hreads = [("hT", sl, kc) for kc in range(16)]
                for s in range(12):
                    prefetch(t, s)
                    if s == 7 and t + 1 < NT:
                        norm_tile(t + 1)
                    ws = wslots[(t, s)]
                    kind = s // 2
                    if kind == 2:
                        for b in range(4):
                            bank = cnt["ps"] % NPSB
                            cnt["ps"] += 1

                            def mm(e, b=b, bank=bank, ws=ws):
                                for kc in range(16):
                                    ins = e.matmul(PS(bank), lhsT=hT[sl][:, kc, b * 128:(b + 1) * 128],
                                                   rhs=wr[ws][:, kc, :], start=(kc == 0), stop=(kc == 15))
                                return ins
                            P.op("pe", mm, reads=hreads + [("wr", ws)], writes=[("ps", bank)])
                            sb = cnt["sb"] % NSTG
                            cnt["sb"] += 1
                            P.op("act", lambda e, bank=bank, sb=sb: e.copy(out=stgb[sb], in_=PS(bank)),
                                 reads=[("ps", bank)], writes=[("stgb", sb)])
                            ch = t * 4 + b
                            c0 = (s - 4) * 512
                            P.op("sp", lambda e, sb=sb, ch=ch, c0=c0: e.dma_start(out=v_s[ch, :, c0:c0 + 512], in_=stgb[sb]),
                                 reads=[("stgb", sb)], chan=("stgb", sb))
                        continue
                    for m in range(4):
                        mt = (s % 2) * 4 + m
                        bank = cnt["ps"] % NPSB
                        cnt["ps"] += 1

                        def mm(e, m=m, bank=bank, ws=ws):
                            for kc in range(16):
                                ins = e.matmul(PS(bank), lhsT=wr[ws][:, kc, m * 128:(m + 1) * 128],
                                               rhs=hT[sl][:, kc, :], start=(kc == 0), stop=(kc == 15))
                            return ins
                        P.op("pe", mm, reads=hreads + [("wr", ws)], writes=[("ps", bank)])
                        if kind in (0, 1):
                            sc = 1.0 if kind == 0 else SC
                            ti = cnt["tm"] % 3
                            cnt["tm"] += 1
                            P.op("act", lambda e, bank=bank, ti=ti, sc=sc: e.activation(
                                out=tmp1[ti], in_=PS(bank), func=AF.Copy, scale=sc),
                                reads=[("ps", bank)], writes=[("tmp1", ti)])
                            P.op("dve", lambda e, ti=ti: e.tensor_tensor(
                                out=tmp2[ti][0:64, :], in0=tmp1[ti][64:128, :], in1=ropeS[sl][64:128, :], op=ALU.mult),
                                reads=[("tmp1", ti), ("ropeS", sl)], writes=[("tmp2", ti, 0)])
                            P.op("dve", lambda e, ti=ti: e.tensor_tensor(
                                out=tmp2[ti][64:128, :], in0=tmp1[ti][0:64, :], in1=ropeS[sl][0:64, :], op=ALU.mult),
                                reads=[("tmp1", ti), ("ropeS", sl)], writes=[("tmp2", ti, 1)])
                            P.op("pool", lambda e, ti=ti: e.tensor_tensor(
                                out=tmp1[ti], in0=tmp1[ti], in1=ropeC[sl], op=ALU.mult),
                                reads=[("tmp1", ti), ("ropeC", sl), ("tmp2", ti, 0), ("tmp2", ti, 1)], writes=[("tmp1", ti)])
                            sb = cnt["sb"] % NSTG
                            cnt["sb"] += 1
                            P.op("dve", lambda e, ti=ti, sb=sb: e.tensor_tensor(
                                out=stgb[sb], in0=tmp1[ti], in1=tmp2[ti], op=ALU.add),
                                reads=[("tmp1", ti), ("tmp2", ti, 0), ("tmp2", ti, 1)], writes=[("stgb", sb)])
                            dst = (q_s if kind == 0 else k_s)[t, :, mt, :]
                            P.op("sp", lambda e, sb=sb, dst=dst: e.dma_start(out=dst, in_=stgb[sb]),
                                 reads=[("stgb", sb)], chan=("stgb", sb))
                        elif kind in (3, 5):
                            fnc = AF.Silu if kind == 3 else AF.Gelu_apprx_tanh
                            sb = cnt["sb"] % NSTG
                            cnt["sb"] += 1
                            P.op("act", lambda e, bank=bank, sb=sb, fnc=fnc: e.activation(
                                out=stgb[sb], in_=PS(bank), func=fnc),
                                reads=[("ps", bank)], writes=[("stgb", sb)])
                            dst = (g_s if kind == 3 else gy_s)[t, :, mt, :]
                            P.op("sp", lambda e, sb=sb, dst=dst: e.dma_start(out=dst, in_=stgb[sb]),
                                 reads=[("stgb", sb)], chan=("stgb", sb))
                        else:
                            sf = cnt["sf"] % 4
                            cnt["sf"] += 1
                            P.op("dve", lambda e, bank=bank, sf=sf: e.tensor_copy(out=stgf[sf], in_=PS(bank)),
                                 reads=[("ps", bank)], writes=[("stgf", sf)])
                            dst = ux_s[:, mt, t * 512:(t + 1) * 512]
                            P.op("sp", lambda e, sf=sf, dst=dst: e.dma_start(out=dst, in_=stgf[sf]),
                                 reads=[("stgf", sf)], chan=("stgf", sf))
            for t in range(NT):
                do_tile(t)
            P.barrier()


        def phase_b1(l):
            A.reset()
            Dtab = A.f32([8, 128])
            kftab = A.f32([8, 128])
            qftab = A.f32([8, 128])
            gw = A.bf([4, 8, 128])
            qT = A.bf([8, 512])
            kT = A.bf([8, 512])
            vt = A.bf([4, 1024])
            yp = A.f32([8, 512])
            kf = A.bf([8, 128])
            Pm = A.bf([8, 128])
            qf = A.bf([8, 128])
            sf = A.f32([8, 128])
            sfb = A.bf([8, 128])
            uxw = A.f32([8, 515])
            uc = A.f32([8, 512])
            ucb = A.bf([8, 512])
            rr = A.f32([8, 512])
            ig = A.f32([8, 512])
            aa = A.f32([8, 512])
            hf = A.f32([8, 512])
            carry = A.f32([8])
            PST = PS(0).bitcast(BF16)
            PSS = PS(1, 2).rearrange("p (h i) -> p h i", i=128)
            PSY = PS(3, 2).rearrange("p (h i) -> p h i", i=128)
            fl = lambda ap: ap.rearrange("p h i -> p (h i)")

            P.op("sp", lambda e: e.dma_start(out=fl(Dtab), in_=cst_d[0]), writes=["Dtab"], chan="tb0")
            P.op("sp", lambda e: e.dma_start(out=fl(kftab), in_=cst_d[1]), writes=["kftab"], chan="tb1")
            P.op("sp", lambda e: e.dma_start(out=fl(qftab), in_=cst_d[3]), writes=["qftab"], chan="tb2")
            for k4 in range(4):
                P.op("pool", lambda e, k4=k4: e.dma_start(out=gw[:, k4], in_=lruw[l, k4].rearrange("g i j -> i g j")),
                     writes=[("gw", k4)], chan="gw")
            P.op("dve", lambda e: e.memset(fl(sf), 0.0), writes=["sf"])
            P.op("dve", lambda e: e.memset(fl(sfb), 0.0), writes=["sfb"])
            P.op("dve", lambda e: e.memset(carry, 0.0), writes=["carry"])
            gwr = [("gw", k4) for k4 in range(4)]

            def do_tile(t):
                Rl, Ul = [], []
                R = lambda *a, **k: Rl.append((a, k))
                U = lambda *a, **k: Ul.append((a, k))
                R("sp", lambda e: e.dma_start(out=qT, in_=q_s[t]), writes=["qT"], chan="ldq")
                R("sp", lambda e: e.dma_start(out=kT, in_=k_s[t]), writes=["kT"], chan="ldk")
                R("sp", lambda e: e.dma_start(out=vt, in_=v_s[t * 4:(t + 1) * 4].rearrange("c p e -> p c e")),
                     writes=["vt"], chan="ldv")
                lo = t * 512 - 2
                hi = t * 512 + 513
                wl, wh = 0, 515
                if lo < 0:
                    wl, lo = 2, 0
                    U("dve", lambda e: e.memset(uxw[:, :, 0:2], 0.0), writes=["uxw"])
                if hi > T:
                    wh, hi = 514, T
                    U("dve", lambda e: e.memset(uxw[:, :, 514:515], 0.0), writes=["uxw"])
                U("sp", lambda e: e.dma_start(out=uxw[:, :, wl:wh], in_=ux_s[:, :, lo:hi]), writes=["uxw"], chan="ldu")
                if t == HT:
                    U("dve", lambda e: e.tensor_scalar(out=uxw[:, :, 0:2], in0=uxw[:, :, 0:2], scalar1=linkm[:, 0:1],
                                                          scalar2=None, op0=ALU.mult), reads=["uxw", "linkm"], writes=["uxw"])
                if t == HT - 1:
                    U("dve", lambda e: e.tensor_scalar(out=uxw[:, :, 514:515], in0=uxw[:, :, 514:515], scalar1=linkm[:, 0:1],
                                                          scalar2=None, op0=ALU.mult), reads=["uxw", "linkm"], writes=["uxw"])
                for c in range(4):
                    n = t * 4 + c
                    cs = slice(c * 128, (c + 1) * 128)

                    def trk(e, cs=cs):
                        for h in range(8):
                            ins = e.transpose(out=PST[:, h * 128:(h + 1) * 128], in_=kT[:, h, cs], identity=identB)
                        return ins
                    R("pe", trk, reads=["kT", "identB"], writes=[("ps", 0)])
                    R("dve", lambda e: e.tensor_tensor(out=fl(kf), in0=PST, in1=fl(kftab), op=ALU.mult),
                         reads=[("ps", 0), "kftab"], writes=["kf"])

                    def sc(e, cs=cs):
                        for h in range(8):
                            ins = e.matmul(PSS[:, h, :], lhsT=kT[:, h, cs], rhs=qT[:, h, cs], start=True, stop=True)
                        return ins
                    R("pe", sc, reads=["kT", "qT"], writes=[("ps", 1), ("ps", 2)])
                    R("dve", lambda e: e.tensor_tensor(out=fl(Pm), in0=fl(PSS), in1=fl(Dtab), op=ALU.mult),
                         reads=[("ps", 1), ("ps", 2), "Dtab"], writes=["Pm"])
                    R("pool", lambda e, cs=cs: e.tensor_tensor(out=qf, in0=qT[:, :, cs], in1=qftab, op=ALU.mult),
                         reads=["qT", "qftab"], writes=["qf"])

                    def ymm(e, c=c):
                        for h in range(8):
                            e.matmul(PSY[:, h, :], lhsT=vt[:, c, h * 128:(h + 1) * 128], rhs=Pm[:, h, :], start=True, stop=False)
                            ins = e.matmul(PSY[:, h, :], lhsT=sfb[:, h, :], rhs=qf[:, h, :], start=False, stop=True)
                        return ins
                    R("pe", ymm, reads=["vt", "Pm", "sfb", "qf"], writes=[("ps", 3), ("ps", 4)])
                    R("act", lambda e, cs=cs: e.copy(out=yp[:, :, cs], in_=PSY),
                         reads=[("ps", 3), ("ps", 4)], writes=[("yp", c)])

                    def kvm(e, c=c):
                        for h in range(8):
                            ins = e.matmul(PSS[:, h, :], lhsT=kf[:, h, :], rhs=vt[:, c, h * 128:(h + 1) * 128], start=True, stop=True)
                        return ins
                    R("pe", kvm, reads=["kf", "vt"], writes=[("ps", 1), ("ps", 2)])

                    def upd(e):
                        for h in range(8):
                            ins = e.scalar_tensor_tensor(out=sf[:, h, :], in0=sf[:, h, :], scalar=GCH[h], in1=PSS[:, h, :],
                                                         op0=ALU.mult, op1=ALU.add)
                        return ins
                    R("dve", upd, reads=["sf", ("ps", 1), ("ps", 2)], writes=["sf"])
                    if n == NCH // 2 - 1:
                        R("dve", lambda e: e.tensor_scalar(out=fl(sf), in0=fl(sf), scalar1=linkm[:, 0:1], scalar2=None,
                                                              op0=ALU.mult), reads=["sf", "linkm"], writes=["sf"])
                    R("act", lambda e: e.copy(out=fl(sfb), in_=fl(sf)), reads=["sf"], writes=["sfb"])
                R("sp", lambda e: e.dma_start(out=yp_s[t], in_=yp), reads=[("yp", c) for c in range(4)], chan="sty")
                for blk in range(8):
                    U("act", lambda e, blk=blk: e.activation(out=uc[:, blk, :], in_=uxw[:, blk, 0:512], func=AF.Identity,
                                                                bias=par("cb", l, blk), scale=par("cw0", l, blk)),
                         reads=["uxw", "params"], writes=[("uc", blk)])
                    for tap in (1, 2, 3):
                        U("dve", lambda e, blk=blk, tap=tap: e.scalar_tensor_tensor(
                            out=uc[:, blk, :], in0=uxw[:, blk, tap:tap + 512], scalar=par("cw%d" % tap, l, blk),
                            in1=uc[:, blk, :], op0=ALU.mult, op1=ALU.add),
                            reads=["uxw", "params", ("uc", blk)], writes=[("uc", blk)])
                ucr = [("uc", blk) for blk in range(8)]
                U("act", lambda e: e.copy(out=ucb, in_=uc), reads=ucr, writes=["ucb"])
                gcnt = [0]
                for d in range(2):
                    for k2 in range(2):
                        kind = d * 2 + k2
                        dest = rr if k2 == 0 else ig
                        dname = "rr" if k2 == 0 else "ig"
                        bname = ("baf", "bxf", "bab", "bxb")[kind]
                        for blk in range(8):
                            bank = 5 + gcnt[0] % 3
                            gcnt[0] += 1
                            U("pe", lambda e, kind=kind, blk=blk, bank=bank: e.matmul(
                                PS(bank), lhsT=gw[:, kind, blk, :], rhs=ucb[:, blk, :], start=True, stop=True),
                                reads=["ucb"] + gwr, writes=[("ps", bank)])
                            U("act", lambda e, dest=dest, blk=blk, bank=bank, bname=bname: e.activation(
                                out=dest[:, blk, :], in_=PS(bank), func=AF.Sigmoid, bias=par(bname, l, blk), scale=1.0),
                                reads=[("ps", bank), "params"], writes=[(dname, blk)])
                    for blk in range(8):
                        j = d * L * 8 + l * 8 + blk
                        U("act", lambda e, blk=blk, j=j: e.activation(out=aa[:, blk, :], in_=rr[:, blk, :], func=AF.Exp,
                                                                         scale=c1t[:, j:j + 1]),
                             reads=[("rr", blk), "c1t"], writes=[("aa", blk)])
                        U("act", lambda e, blk=blk, j=j: e.activation(out=rr[:, blk, :], in_=rr[:, blk, :], func=AF.Exp,
                                                                         scale=c2t[:, j:j + 1]),
                             reads=[("rr", blk), "c2t"], writes=[("rr", blk)])
                    rrr = [("rr", blk) for blk in range(8)]
                    igr = [("ig", blk) for blk in range(8)]
                    aar = [("aa", blk) for blk in range(8)]
                    for blk in range(8):
                        U("pool" if blk % 4 == 3 else "dve",
                          lambda e, blk=blk: e.tensor_tensor(out=ig[:, blk, :], in0=ig[:, blk, :], in1=uc[:, blk, :], op=ALU.mult),
                          reads=[("ig", blk), ("uc", blk)], writes=[("ig", blk)])
                    for half in range(2):
                        hs = slice(half * 4, half * 4 + 4)
                        hr = [("rr", blk) for blk in range(half * 4, half * 4 + 4)]
                        U("act", lambda e, hs=hs: e.activation(out=rr[:, hs, :], in_=rr[:, hs, :], func=AF.Sqrt, bias=oneT[:, 0:1], scale=-1.0),
                          reads=hr + ["oneT"], writes=hr)
                    for blk in range(8):
                        U("pool" if blk % 4 == 3 else "dve",
                          lambda e, blk=blk: e.tensor_tensor(out=ig[:, blk, :], in0=ig[:, blk, :], in1=rr[:, blk, :], op=ALU.mult),
                          reads=[("ig", blk), ("rr", blk)], writes=[("ig", blk)])
                    if d == 0:
                        for blk in range(8):
                            U("dve", lambda e, blk=blk: e.tensor_tensor_scan(
                                out=hf[:, blk, :], data0=aa[:, blk, :], data1=ig[:, blk, :],
                                initial=carry[:, blk:blk + 1], op0=ALU.mult, op1=ALU.add),
                              reads=[("aa", blk), ("ig", blk), "carry"], writes=[("hf", blk)])
                        hfr = [("hf", blk) for blk in range(8)]
                        U("dve", lambda e: e.tensor_copy(out=carry, in_=hf[:, :, 511]), reads=hfr, writes=["carry"])
                        if t == HT - 1:
                            U("dve", lambda e: e.tensor_scalar(out=carry, in0=carry, scalar1=linkm[:, 0:1], scalar2=None,
                                                                  op0=ALU.mult), reads=["carry", "linkm"], writes=["carry"])
                        U("sp", lambda e: e.dma_start(out=hf_s[t], in_=hf), reads=hfr, chan="sth")
                    else:
                        U("sp", lambda e: e.dma_start(out=ab_s[t], in_=aa), reads=aar, chan="sta")
                        U("sp", lambda e: e.dma_start(out=bb_s[t], in_=ig), reads=igr, chan="stb")

                for a, k in _merge(Rl, Ul):
                    P.op(*a, **k)

            for t in range(NT):
                do_tile(t)
            P.barrier()

        def phase_b2(l):
            A.reset()
            kbtab = A.f32([8, 128])
            qbtab = A.f32([8, 128])
            qT = A.bf([8, 512])
            kT = A.bf([8, 512])
            vt = A.bf([4, 1024])
            yp = A.f32([8, 512])
            sg = A.bf([8, 512])
            gy = A.bf([8, 512])
            ab = A.f32([8, 512])
            bb = A.f32([8, 512])
            hf = A.f32([8, 512])
            kb = A.bf([8, 128])
            qb = A.bf([8, 128])
            sb = A.f32([8, 128])
            sbb = A.bf([8, 128])
            ysq = A.bf([8, 512])
            mixb = A.bf([16, 512])
            rsn = [A.f32([512]) for _ in range(2)]
            rstd = [A.f32([512]) for _ in range(2)]
            tmpn = [A.f32([512]) for _ in range(2)]
            carry = A.f32([8])
            hsq = A.bf([8, 512])
            rsnU = A.f32([512])
            rstdU = A.f32([512])
            tmpnU = [A.f32([512]) for _ in range(2)]
            PST = PS(0).bitcast(BF16)
            PSS = PS(1, 2).rearrange("p (h i) -> p h i", i=128)
            PSY = PS(3, 2).rearrange("p (h i) -> p h i", i=128)
            fl = lambda ap: ap.rearrange("p h i -> p (h i)")
            rev = lambda ap: bass.AP(ap.tensor, ap.offset + (ap.ap[-1][1] - 1) * ap.ap[-1][0],
                                     [list(x) for x in ap.ap[:-1]] + [[-ap.ap[-1][0], ap.ap[-1][1]]])

            P.op("sp", lambda e: e.dma_start(out=fl(kbtab), in_=cst_d[2]), writes=["kbtab"], chan="tb0")
            P.op("sp", lambda e: e.dma_start(out=fl(qbtab), in_=cst_d[4]), writes=["qbtab"], chan="tb1")
            P.op("dve", lambda e: e.memset(fl(sb), 0.0), writes=["sb"])
            P.op("dve", lambda e: e.memset(fl(sbb), 0.0), writes=["sbb"])
            P.op("dve", lambda e: e.memset(carry, 0.0), writes=["carry"])
            ncnt = [0]

            def do_tile(t):
                Rl, Ul = [], []
                R = lambda *a, **k: Rl.append((a, k))
                U = lambda *a, **k: Ul.append((a, k))
                R("sp", lambda e: e.dma_start(out=qT, in_=q_s[t]), writes=["qT"], chan="ldq")
                R("sp", lambda e: e.dma_start(out=kT, in_=k_s[t]), writes=["kT"], chan="ldk")
                R("sp", lambda e: e.dma_start(out=vt, in_=v_s[t * 4:(t + 1) * 4].rearrange("c p e -> p c e")),
                     writes=["vt"], chan="ldv")
                R("sp", lambda e: e.dma_start(out=yp, in_=yp_s[t]), writes=[("yp", c) for c in range(4)], chan="ldy")
                R("sp", lambda e: e.dma_start(out=sg, in_=g_s[t]), writes=["sg"], chan="ldg")
                U("sp", lambda e: e.dma_start(out=gy, in_=gy_s[t]), writes=["gy"], chan="ldgy")
                U("sp", lambda e: e.dma_start(out=ab, in_=ab_s[t]), writes=["ab"], chan="lda")
                U("sp", lambda e: e.dma_start(out=bb, in_=bb_s[t]), writes=["bb"], chan="ldb")
                U("sp", lambda e: e.dma_start(out=hf, in_=hf_s[t]), writes=["hf"], chan="ldh")
                for c in (3, 2, 1, 0):
                    n = t * 4 + c
                    cs = slice(c * 128, (c + 1) * 128)

                    def trk(e, cs=cs):
                        for h in range(8):
                            ins = e.transpose(out=PST[:, h * 128:(h + 1) * 128], in_=kT[:, h, cs], identity=identB)
                        return ins
                    R("pe", trk, reads=["kT", "identB"], writes=[("ps", 0)])
                    R("dve", lambda e: e.tensor_tensor(out=fl(kb), in0=PST, in1=fl(kbtab), op=ALU.mult),
                         reads=[("ps", 0), "kbtab"], writes=["kb"])
                    R("pool", lambda e, cs=cs: e.tensor_tensor(out=qb, in0=qT[:, :, cs], in1=qbtab, op=ALU.mult),
                         reads=["qT", "qbtab"], writes=["qb"])

                    def ymm(e):
                        for h in range(8):
                            ins = e.matmul(PSY[:, h, :], lhsT=sbb[:, h, :], rhs=qb[:, h, :], start=True, stop=True)
                        return ins
                    R("pe", ymm, reads=["sbb", "qb"], writes=[("ps", 3), ("ps", 4)])
                    R("dve", lambda e, cs=cs: e.tensor_tensor(out=yp[:, :, cs], in0=yp[:, :, cs], in1=PSY, op=ALU.add),
                         reads=[("ps", 3), ("ps", 4), ("yp", c)], writes=[("yp", c)])

                    def kvm(e, c=c):
                        for h in range(8):
                            ins = e.matmul(PSS[:, h, :], lhsT=kb[:, h, :], rhs=vt[:, c, h * 128:(h + 1) * 128], start=True, stop=True)
                        return ins
                    R("pe", kvm, reads=["kb", "vt"], writes=[("ps", 1), ("ps", 2)])

                    def upd(e):
                        for h in range(8):
                            ins = e.scalar_tensor_tensor(out=sb[:, h, :], in0=sb[:, h, :], scalar=GCH[h], in1=PSS[:, h, :],
                                                         op0=ALU.mult, op1=ALU.add)
                        return ins
                    R("dve", upd, reads=["sb", ("ps", 1), ("ps", 2)], writes=["sb"])
                    if n == NCH // 2:
                        R("dve", lambda e: e.tensor_scalar(out=fl(sb), in0=fl(sb), scalar1=linkm[:, 0:1], scalar2=None,
                                                              op0=ALU.mult), reads=["sb", "linkm"], writes=["sb"])
                    R("act", lambda e: e.copy(out=fl(sbb), in_=fl(sb)), reads=["sb"], writes=["sbb"])
                ypr = [("yp", c) for c in range(4)]
                R("act", lambda e: e.activation(out=ysq, in_=yp, func=AF.Square), reads=ypr, writes=["ysq"])
                for h in range(8):
                    bank = 5 + ncnt[0] % 2
                    ri = ncnt[0] % 2
                    ncnt[0] += 1
                    R("pe", lambda e, h=h, bank=bank: e.matmul(PS(bank), lhsT=onesB, rhs=ysq[:, h, :], start=True, stop=True),
                         reads=["ysq", "onesB"], writes=[("ps", bank)])
                    R("act", lambda e, bank=bank, ri=ri: e.activation(out=rsn[ri], in_=PS(bank), func=AF.Ln,
                                                                         bias=epsT[:, 0:1], scale=1.0 / 128.0),
                         reads=[("ps", bank), "epsT"], writes=[("rsn", ri)])
                    R("act", lambda e, ri=ri: e.activation(out=rstd[ri], in_=rsn[ri], func=AF.Exp, scale=-0.5),
                         reads=[("rsn", ri)], writes=[("rstd", ri)])
                    R("pool", lambda e, h=h, ri=ri: e.tensor_tensor(out=tmpn[ri], in0=yp[:, h, :], in1=rstd[ri], op=ALU.mult),
                         reads=ypr + [("rstd", ri)], writes=[("tmpn", ri)])
                    R("dve", lambda e, h=h, ri=ri: e.scalar_tensor_tensor(
                        out=mixb[:, h, :], in0=tmpn[ri], scalar=par("retn", l, h), in1=sg[:, h, :], op0=ALU.mult, op1=ALU.mult),
                        reads=[("tmpn", ri), "sg", "params"], writes=[("mixb", h)])
                def scan(e):
                    for blk in range(8):
                        ins = e.tensor_tensor_scan(out=rev(bb[:, blk, :]), data0=rev(ab[:, blk, :]), data1=rev(bb[:, blk, :]),
                                                   initial=carry[:, blk:blk + 1], op0=ALU.mult, op1=ALU.add)
                    return ins
                U("dve", scan, reads=["ab", "bb", "carry"], writes=["bb"])
                U("dve", lambda e: e.tensor_copy(out=carry, in_=bb[:, :, 0]), reads=["bb"], writes=["carry"])
                if t == HT:
                    U("dve", lambda e: e.tensor_scalar(out=carry, in0=carry, scalar1=linkm[:, 0:1], scalar2=None,
                                                          op0=ALU.mult), reads=["carry", "linkm"], writes=["carry"])
                U("pool", lambda e: e.tensor_tensor(out=hf, in0=hf, in1=bb, op=ALU.add), reads=["hf", "bb"], writes=["hf"])
                U("act", lambda e: e.activation(out=hsq, in_=hf, func=AF.Square), reads=["hf"], writes=["hsq"])
                bank = 7

                def lsum(e):
                    for blk in range(8):
                        ins = e.matmul(PS(bank), lhsT=onesB, rhs=hsq[:, blk, :], start=(blk == 0), stop=(blk == 7))
                    return ins
                U("pe", lsum, reads=["hsq", "onesB"], writes=[("ps", bank)])
                U("act", lambda e: e.activation(out=rsnU, in_=PS(bank), func=AF.Ln, bias=epsT[:, 0:1], scale=1.0 / 1024.0),
                     reads=[("ps", bank), "epsT"], writes=["rsnU"])
                U("act", lambda e: e.activation(out=rstdU, in_=rsnU, func=AF.Exp, scale=-0.5), reads=["rsnU"], writes=["rstdU"])
                for blk in range(8):
                    ti = blk % 2
                    U("pool", lambda e, blk=blk, ti=ti: e.tensor_tensor(out=tmpnU[ti], in0=hf[:, blk, :], in1=rstdU, op=ALU.mult),
                         reads=["hf", "rstdU"], writes=[("tmpnU", ti)])
                    U("dve", lambda e, blk=blk, ti=ti: e.scalar_tensor_tensor(
                        out=mixb[:, 8 + blk, :], in0=tmpnU[ti], scalar=par("lrun", l, blk), in1=gy[:, blk, :],
                        op0=ALU.mult, op1=ALU.mult),
                        reads=[("tmpnU", ti), "gy", "params"], writes=[("mixb", 8 + blk)])
                for a, k in _merge(Rl, Ul):
                    P.op(*a, **k)

                P.op("sp", lambda e: e.dma_start(out=mix_s[t], in_=mixb), reads=[("mixb", j) for j in range(16)], chan="stm")

            for t in range(NT - 1, -1, -1):
                do_tile(t)
            P.barrier()

        def phase_c(l, last):
            A.reset()
            B16 = A.bf([16, 512])
            xC = A.f32([16, 512])
            ob = A.f32([16, 512])
            act = A.bf([44, 512])
            NR = 4
            ring = [A.bf([5632]) for _ in range(NR)]
            sgt = [A.f32([512]) for _ in range(2)]
            sqs = [A.bf([512]) for _ in range(4)]
            tmpx = [A.f32([512]) for _ in range(3)]
            rsn = A.f32([512])
            rstd = A.f32([512])
            NB = 6
            st = {"ld": 0, "use": 0, "ps": 0, "sg": 0, "sq": 0, "tx": 0}
            loads = []
            for t in range(NT):
                loads += [("out", s) for s in range(8)]
                for j in range(22):
                    loads += [("gate", j), ("up", j)]
                loads += [("down", s) for s in range(16)]
            srcs = {"out": wb_out, "gate": wb_gate, "up": wb_up, "down": wb_down}

            def issue_load():
                i = st["ld"]
                if i >= len(loads):
                    return
                st["ld"] += 1
                kind, s = loads[i]
                slot = i % NR
                src = srcs[kind][l, s].rearrange("p a b -> p (a b)")
                n = 44 * 128 if kind == "down" else 16 * 256
                P.op("sp", lambda e: e.dma_start(out=ring[slot][:, 0:n], in_=src),
                     reads=[("wb", kind, l)], writes=[("ring", slot)], chan=("ring", slot))

            def next_slab():
                i = st["use"]
                st["use"] += 1
                return i % NR

            for _ in range(NR):
                issue_load()

            def norm_from(bank_reads):
                P.op("act", lambda e: e.activation(out=rsn, in_=PS(6), func=AF.Ln, bias=epsT[:, 0:1], scale=1.0 / D),
                     reads=[("ps", 6), "epsT"], writes=["rsn"])
                P.op("act", lambda e: e.activation(out=rstd, in_=rsn, func=AF.Exp, scale=-0.5), reads=["rsn"], writes=["rstd"])

            pend = []

            def flush_sq(keep):
                while len(pend) > keep:
                    qi, mt = pend.pop(0)
                    P.op("pe", lambda e, qi=qi, mt=mt: e.matmul(PS(6), lhsT=onesB, rhs=sqs[qi], start=(mt == 0), stop=(mt == 15)),
                         reads=[("sqs", qi), "onesB"], writes=[("ps", 6)])

            def proj_epilogue(bank, mt):
                P.op("act", lambda e: e.copy(out=ob[:, mt, :], in_=PS(bank)), reads=[("ps", bank)], writes=[("ob", mt)])
                qi = st["sq"] % 4
                st["sq"] += 1
                P.op("act", lambda e: e.activation(out=sqs[qi], in_=PS(bank), func=AF.Square),
                     reads=[("ps", bank)], writes=[("sqs", qi)])
                pend.append((qi, mt))

            def residual(gname, sqdst=None):
                for kc in range(16):
                    ti = st["tx"] % 3
                    st["tx"] += 1
                    if sqdst is not None and kc >= 2:
                        k2 = kc - 2
                        P.op("pe", lambda e, k2=k2: e.matmul(PS(6), lhsT=onesB, rhs=sqdst[:, k2, :], start=(k2 == 0), stop=False),
                             reads=[("B16", k2), "onesB"], writes=[("ps", 6)])
                    P.op("dve", lambda e, kc=kc, ti=ti: e.scalar_tensor_tensor(
                        out=tmpx[ti], in0=ob[:, kc, :], scalar=par(gname, l, kc), in1=rstd, op0=ALU.mult, op1=ALU.mult),
                        reads=[("ob", kc), "rstd", "params"], writes=[("tmpx", ti)])
                    P.op("pool" if kc % 3 == 0 else "dve",
                         lambda e, kc=kc, ti=ti: e.tensor_tensor(out=xC[:, kc, :], in0=xC[:, kc, :], in1=tmpx[ti], op=ALU.add),
                         reads=[("xC", kc), ("tmpx", ti)], writes=[("xC", kc)])
                    if sqdst is not None:
                        P.op("act", lambda e, kc=kc: e.activation(out=sqdst[:, kc, :], in_=xC[:, kc, :], func=AF.Square),
                             reads=[("xC", kc)], writes=[("B16", kc)])
                if sqdst is not None:
                    for k2 in (14, 15):
                        P.op("pe", lambda e, k2=k2: e.matmul(PS(6), lhsT=onesB, rhs=sqdst[:, k2, :], start=False, stop=(k2 == 15)),
                             reads=[("B16", k2), "onesB"], writes=[("ps", 6)])

            def do_tile(t):
                b16r = [("B16", kc) for kc in range(16)]
                xcr = [("xC", kc) for kc in range(16)]
                if t == 0:
                    P.op("sp", lambda e: e.dma_start(out=B16, in_=mix_s[t]), writes=b16r, chan="ldm")
                P.op("sp", lambda e: e.dma_start(out=xC, in_=xs[t]), writes=xcr, chan="ldx")
                for s in range(8):
                    slot = next_slab()
                    wv = ring[slot][:, 0:4096].rearrange("p (a b) -> p a b", b=256)
                    for m in range(2):
                        mt = s * 2 + m
                        bank = st["ps"] % NB
                        st["ps"] += 1

                        def mm(e, m=m, bank=bank, wv=wv):
                            for kc in range(16):
                                ins = e.matmul(PS(bank), lhsT=wv[:, kc, m * 128:(m + 1) * 128], rhs=B16[:, kc, :],
                                               start=(kc == 0), stop=(kc == 15))
                            return ins
                        P.op("pe", mm, reads=b16r + [("ring", slot)], writes=[("ps", bank)])
                        flush_sq(2)
                        proj_epilogue(bank, mt)
                    issue_load()
                flush_sq(0)
                norm_from(None)
                residual("npo", sqdst=B16)
                norm_from(None)
                for kc in range(16):
                    P.op("dve", lambda e, kc=kc: e.scalar_tensor_tensor(
                        out=B16[:, kc, :], in0=xC[:, kc, :], scalar=par("npf", l, kc), in1=rstd, op0=ALU.mult, op1=ALU.mult),
                        reads=[("xC", kc), "rstd", "params"], writes=[("B16", kc)])
                for j in range(22):
                    sg_ = next_slab()
                    su_ = next_slab()
                    wg = ring[sg_][:, 0:4096].rearrange("p (a b) -> p a b", b=256)
                    wu = ring[su_][:, 0:4096].rearrange("p (a b) -> p a b", b=256)
                    for m in range(2):
                        ft = j * 2 + m
                        bg = st["ps"] % NB
                        st["ps"] += 1
                        bu = st["ps"] % NB
                        st["ps"] += 1

                        def mmg(e, m=m, bg=bg, wg=wg):
                            for kc in range(16):
                                ins = e.matmul(PS(bg), lhsT=wg[:, kc, m * 128:(m + 1) * 128], rhs=B16[:, kc, :],
                                               start=(kc == 0), stop=(kc == 15))
                            return ins

                        def mmu(e, m=m, bu=bu, wu=wu):
                            for kc in range(16):
                                ins = e.matmul(PS(bu), lhsT=wu[:, kc, m * 128:(m + 1) * 128], rhs=B16[:, kc, :],
                                               start=(kc == 0), stop=(kc == 15))
                            return ins
                        P.op("pe", mmg, reads=b16r + [("ring", sg_)], writes=[("ps", bg)])
                        P.op("pe", mmu, reads=b16r + [("ring", su_)], writes=[("ps", bu)])
                        si = st["sg"] % 2
                        st["sg"] += 1
                        P.op("act", lambda e, bg=bg, si=si: e.activation(out=sgt[si], in_=PS(bg), func=AF.Silu),
                             reads=[("ps", bg)], writes=[("sgt", si)])
                        P.op("dve", lambda e, bu=bu, si=si, ft=ft: e.tensor_tensor(out=act[:, ft, :], in0=sgt[si], in1=PS(bu), op=ALU.mult),
                             reads=[("ps", bu), ("sgt", si)], writes=[("act", ft)])
                    issue_load()
                    issue_load()
                if t + 1 < NT:
                    P.op("sp", lambda e: e.dma_start(out=B16, in_=mix_s[t + 1]), writes=b16r, chan="ldm")
                actr = [("act", ft) for ft in range(44)]
                for s in range(16):
                    slot = next_slab()
                    wd = ring[slot][:, 0:5632].rearrange("p (a b) -> p a b", b=128)
                    bank = st["ps"] % NB
                    st["ps"] += 1

                    def mmd(e, bank=bank, wd=wd):
                        for kf_ in range(44):
                            ins = e.matmul(PS(bank), lhsT=wd[:, kf_, :], rhs=act[:, kf_, :], start=(kf_ == 0), stop=(kf_ == 43))
                        return ins
                    P.op("pe", mmd, reads=actr + [("ring", slot)], writes=[("ps", bank)])
                    flush_sq(1)
                    proj_epilogue(bank, s)
                    issue_load()
                flush_sq(0)
                norm_from(None)
                residual("npff")
                P.op("sp", lambda e: e.dma_start(out=xs[t], in_=xC), reads=xcr, chan="stx")

            for t in range(NT):
                do_tile(t)
            P.barrier()

        def phase_tout():
            A.reset()
            xt = [A.f32([16, 512]) for _ in range(2)]
            yo = [A.f32([4, 2048]) for _ in range(2)]
            cnt = [0]

            def do_tile(t):
                sl = t % 2
                P.op("sp", lambda e: e.dma_start(out=xt[sl], in_=xs[t]), writes=[("xt", sl)], chan=("ldxt", sl))
                for b in range(4):
                    for kq in range(4):
                        bank = cnt[0] % 6
                        cnt[0] += 1

                        def tr(e, b=b, kq=kq, bank=bank):
                            for k4 in range(4):
                                kc = kq * 4 + k4
                                ins = e.transpose(out=PS(bank)[:, k4 * 128:(k4 + 1) * 128],
                                                  in_=xt[sl][:, kc, b * 128:(b + 1) * 128], identity=identF)
                            return ins
                        P.op("pe", tr, reads=[("xt", sl), "identF"], writes=[("ps", bank)])
                        if (b * 4 + kq) % 2 == 0:
                            P.op("act", lambda e, b=b, kq=kq, bank=bank: e.copy(out=yo[sl][:, b, kq * 512:(kq + 1) * 512], in_=PS(bank)),
                                 reads=[("ps", bank)], writes=[("yo", sl, b, kq)])
                        else:
                            P.op("dve", lambda e, b=b, kq=kq, bank=bank: e.tensor_copy(out=yo[sl][:, b, kq * 512:(kq + 1) * 512], in_=PS(bank)),
                                 reads=[("ps", bank)], writes=[("yo", sl, b, kq)])
                dst = y_out[t * 512:(t + 1) * 512, :].rearrange("(b p) d -> p b d", p=128)
                P.op("sp", lambda e: e.dma_start(out=dst, in_=yo[sl]),
                     reads=[("yo", sl, b, kq) for b in range(4) for kq in range(4)], chan=("sty", sl))

            for t in range(NT):
                do_tile(t)
            P.barrier()

        phase_tin()
        stages = []
        for l in range(L):
            stages += [("a", l), ("b1", l), ("b2", l), ("c", l)]
        done = False
        if stop_after == "tin":
            done = True
        for kind, l in stages:
            if done:
                break
            if kind == "a":
                phase_a(l)
            elif kind == "b1":
                phase_b1(l)
            elif kind == "b2":
                phase_b2(l)
            else:
                if l + 1 < L:
                    convert_layer(l + 1)
                phase_c(l, l == L - 1)
            if stop_after == (kind, l):
                done = True
        if not done:
            phase_tout()
        P.barrier()

        block = es.enter_context(nc.Block())
        P.emit(nc, block, sems_ctx)
    return nc, outs, P


def _prep_inputs(inp, L, S):
    T = 2 * S
    ropes, cst, _ = _host_consts(S)
    params = _pack_params(inp, L)
    lruw = np.ascontiguousarray(np.stack([np.asarray(inp[k], np.float32)[:L] for k in
                                          ("lru_wa_fwd", "lru_wx_fwd", "lru_wa_bwd", "lru_wx_bwd")], 1))
    ident = np.eye(128, dtype=np.float32)
    shared = {
        "w_in": np.ascontiguousarray(np.asarray(inp["w_in"], np.float32)[:L]),
        "w_out": np.ascontiguousarray(np.asarray(inp["w_out"], np.float32)[:L]),
        "w_gate": np.ascontiguousarray(np.asarray(inp["w_gate"], np.float32)[:L]),
        "w_up": np.ascontiguousarray(np.asarray(inp["w_up"], np.float32)[:L]),
        "w_down": np.ascontiguousarray(np.asarray(inp["w_down"], np.float32)[:L]),
        "lruw": lruw, "params": params, "cst": cst, "ident": ident,
    }
    xp = np.asarray(inp["x_prompt"], np.float32)
    xsm = np.asarray(inp["x_sample"], np.float32)
    maps = []
    for c in range(NCORES):
        m = dict(shared)
        if c < 4:
            m["x"] = np.ascontiguousarray(xsm[c])
            m["rope"] = ropes[0]
            m["link"] = np.ones((128, 1), np.float32)
        else:
            i = c - 4
            m["x"] = np.ascontiguousarray(xp[2 * i:2 * i + 2].reshape(T, D))
            m["rope"] = ropes[1]
            m["link"] = np.zeros((128, 1), np.float32)
        maps.append(m)
    return maps


def run(inp, L, dbg=(), stop_after=None, trace=False, cores=None, verbose=False):
    import time
    S = int(np.asarray(inp["x_prompt"]).shape[1])
    assert np.asarray(inp["x_sample"]).shape[1] == 2 * S
    t0 = time.time()
    nc, outs, P = build_program(L, S, dbg=dbg, stop_after=stop_after)
    t1 = time.time()
    maps = _prep_inputs(inp, L, S)
    if cores is not None:
        maps = [maps[c] for c in cores]
    t2 = time.time()
    res = run_bass_kernel_spmd(nc, maps, core_ids=list(range(len(maps))), trace=trace)
    t3 = time.time()
    if verbose:
        print("build %.1fs prep %.1fs run %.1fs ops=%d sems=%d" % (t1 - t0, t2 - t1, t3 - t2, len(P.ops), P.n_sems), flush=True)
    extra = {n: [res.results[c][n] for c in range(len(maps))] for n in outs}
    if cores is not None:
        return [res.results[c]["y"] for c in range(len(maps))], extra, res
    ys = [res.results[c]["y"] for c in range(NCORES)]
    y_sample = np.stack(ys[:4], 0)
    y_prompt = np.concatenate([ys[4 + i].reshape(2, S, D) for i in range(4)], 0)
    return (y_prompt, y_sample), extra, res


def kernel(**inputs):
    (yp, ys), _, _ = run(inputs, 4)
    return yp, ys
```
